# Optimizing a Trainium2 kernel written in Bass

```python
import jax, jax.numpy as jnp
from jax import lax
import numpy as np

D_MODEL = 1024
BATCH = 32
SEQ = 2048
DEPTH = 4

HEAD_DIM = 64
MIX_W = D_MODEL
A_HEADS = MIX_W // 2 // HEAD_DIM
A_KV_HEADS = A_HEADS // 4
B_HEADS = MIX_W // 2 // HEAD_DIM
C_WIDTH = MIX_W // 2
D_WIDTH = MIX_W // 2
C_CONV = 3
D_CONV = 31
D_FF = 2816
GRID_W = 64
NA_ROWS = 8
NA_COLS = 16
Q_BLOCK = 128
ROPE_THETA = 10000.0
EPS = 1e-6
NEG_INF = -1e30
N_EVEN = (DEPTH + 1) // 2
N_ODD = DEPTH // 2
A_Q = A_HEADS * HEAD_DIM
A_KV = A_KV_HEADS * HEAD_DIM
B_W = B_HEADS * HEAD_DIM
AB_IN = A_Q + 2 * A_KV + 3 * B_W
AB_OUT = A_Q + B_W
CD_IN = 3 * C_WIDTH + 2 * D_WIDTH
CD_OUT = C_WIDTH + D_WIDTH

kernel_name = "hybrid_gqa_natten_shortconv_conformer_encoder"


def rms_norm(x, g):
    xf = x.astype(jnp.float32)
    y = xf * lax.rsqrt(jnp.mean(xf * xf, axis=-1, keepdims=True) + EPS)
    return (y * g.astype(jnp.float32)).astype(x.dtype)


def layer_norm(x, g, b):
    xf = x.astype(jnp.float32)
    mu = jnp.mean(xf, axis=-1, keepdims=True)
    var = jnp.mean(jnp.square(xf - mu), axis=-1, keepdims=True)
    y = (xf - mu) * lax.rsqrt(var + EPS)
    return (y * g.astype(jnp.float32) + b.astype(jnp.float32)).astype(x.dtype)


def swiglu(x, w_gate, w_up, w_down):
    return (jax.nn.silu(x @ w_gate) * (x @ w_up)) @ w_down


def axial_rope(seq):
    t = jnp.arange(seq)
    row = (t // GRID_W).astype(jnp.float32)
    col = (t % GRID_W).astype(jnp.float32)
    half = HEAD_DIM // 2
    freqs = ROPE_THETA ** (-jnp.arange(0, half, 2, dtype=jnp.float32) / half)
    ang = jnp.concatenate([row[:, None] * freqs, col[:, None] * freqs], axis=-1)
    return jnp.cos(ang), jnp.sin(ang)


def apply_rope(x, cos, sin):
    xf = x.astype(jnp.float32).reshape(*x.shape[:-1], x.shape[-1] // 2, 2)
    x0, x1 = xf[..., 0], xf[..., 1]
    c = cos[None, :, None, :]
    s = sin[None, :, None, :]
    out = jnp.stack([x0 * c - x1 * s, x0 * s + x1 * c], axis=-1)
    return out.reshape(x.shape).astype(x.dtype)


def depthwise_conv(u, w):
    k = w.shape[0]
    return lax.conv_general_dilated(
        u, w[:, None, :].astype(u.dtype), window_strides=(1,),
        padding=[(k // 2, k // 2)], dimension_numbers=("NWC", "WIO", "NWC"),
        feature_group_count=u.shape[-1])


def global_gqa(q, k, v):
    b, s, ha, d = q.shape
    kv = k.shape[2]
    g = ha // kv
    nqb = s // Q_BLOCK
    scale = d ** -0.5
    qb = q.reshape(b, nqb, Q_BLOCK, kv, g, d).transpose(1, 0, 2, 3, 4, 5)

    def block(qi):
        sc = jnp.einsum("bqkgd,bskd->bkgqs", qi, k).astype(jnp.float32) * scale
        p = jax.nn.softmax(sc, axis=-1).astype(v.dtype)
        return jnp.einsum("bkgqs,bskd->bqkgd", p, v)

    o = lax.map(block, qb)
    return o.transpose(1, 0, 2, 3, 4, 5).reshape(b, s, ha * d)


def neighborhood_attention(q, k, v, rpb):
    b, s, h, d = q.shape
    rows = s // GRID_W
    wr = min(NA_ROWS, rows)
    wc = NA_COLS
    kw = 2 * wc
    ncb = GRID_W // wc
    scale = d ** -0.5
    qg = q.reshape(b, rows, GRID_W, h, d)
    kg = k.reshape(b, rows, GRID_W, h, d)
    vg = v.reshape(b, rows, GRID_W, h, d)
    qcol = jnp.arange(GRID_W).reshape(ncb, wc)
    kcol = jnp.clip(jnp.arange(ncb) * wc - wc // 2, 0, GRID_W - kw)[:, None] + jnp.arange(kw)
    wcs = jnp.clip(qcol - wc // 2, 0, GRID_W - wc)
    kc = kcol[:, None, :]
    col_mask = (kc >= wcs[..., None]) & (kc < wcs[..., None] + wc)
    col_off = jnp.clip(kc - qcol[..., None], -(NA_COLS - 1), NA_COLS - 1) + NA_COLS - 1
    rpb_f = rpb.astype(jnp.float32)

    def row_block(r):
        rs = jnp.clip(r - wr // 2, 0, rows - wr)
        kb = lax.dynamic_slice_in_dim(kg, rs, wr, axis=1)[:, :, kcol]
        vb = lax.dynamic_slice_in_dim(vg, rs, wr, axis=1)[:, :, kcol]
        qr = lax.dynamic_index_in_dim(qg, r, axis=1, keepdims=False).reshape(b, ncb, wc, h, d)
        sc = jnp.einsum("bnqhd,bwnkhd->bhnqwk", qr, kb).astype(jnp.float32) * scale
        row_off = rs + jnp.arange(wr) - r + NA_ROWS - 1
        bias = rpb_f[:, row_off[None, None, :, None], col_off[:, :, None, :]]
        sc = jnp.where(col_mask[:, :, None, :], sc + bias, NEG_INF)
        p = jax.nn.softmax(sc.reshape(b, h, ncb, wc, wr * kw), axis=-1)
        p = p.reshape(sc.shape).astype(v.dtype)
        return jnp.einsum("bhnqwk,bwnkhd->bnqhd", p, vb).reshape(b, GRID_W, h, d)

    o = lax.map(row_block, jnp.arange(rows))
    return o.transpose(1, 0, 2, 3, 4).reshape(b, s, h * d)


def mixer_ab(hx, w_in, w_out, q_norm, k_norm, rpb, cos, sin):
    b, s, _ = hx.shape
    u = hx @ w_in
    aq, ak, av, bq, bk, bv = jnp.split(
        u, [A_Q, A_Q + A_KV, A_Q + 2 * A_KV, A_Q + 2 * A_KV + B_W, A_Q + 2 * A_KV + 2 * B_W], axis=-1)
    aq = apply_rope(rms_norm(aq.reshape(b, s, A_HEADS, HEAD_DIM), q_norm), cos, sin)
    ak = apply_rope(rms_norm(ak.reshape(b, s, A_KV_HEADS, HEAD_DIM), k_norm), cos, sin)
    ya = global_gqa(aq, ak, av.reshape(b, s, A_KV_HEADS, HEAD_DIM))
    yb = neighborhood_attention(bq.reshape(b, s, B_HEADS, HEAD_DIM),
                                bk.reshape(b, s, B_HEADS, HEAD_DIM),
                                bv.reshape(b, s, B_HEADS, HEAD_DIM), rpb)
    return jnp.concatenate([ya, yb], axis=-1) @ w_out


def mixer_cd(hx, w_in, w_out, c_conv_w, d_conv_w, d_norm_g, d_norm_b):
    u = hx @ w_in
    c_h, c_b, c_c, d_a, d_g = jnp.split(
        u, [C_WIDTH, 2 * C_WIDTH, 3 * C_WIDTH, 3 * C_WIDTH + D_WIDTH], axis=-1)
    yc = c_b * depthwise_conv(c_c * c_h, c_conv_w)
    yd = depthwise_conv(d_a * jax.nn.sigmoid(d_g), d_conv_w)
    yd = jax.nn.silu(layer_norm(yd, d_norm_g, d_norm_b))
    return jnp.concatenate([yc, yd], axis=-1) @ w_out


def setup_inputs(seed: int = 0) -> dict:
    key = jax.random.key(seed)
    ks = jax.random.split(key, 20)
    f32 = jnp.float32

    def nrm(k, shape, fan_in):
        return jax.random.normal(k, shape, f32) * (fan_in ** -0.5)

    def gain(k, shape):
        return 1.0 + 0.02 * jax.random.normal(k, shape, f32)

    return {
        "x": jax.random.normal(ks[0], (BATCH, SEQ, D_MODEL), f32),
        "ffn_norm": gain(ks[1], (DEPTH, 2, D_MODEL)),
        "mix_norm": gain(ks[2], (DEPTH, D_MODEL)),
        "ffn_w_gate": nrm(ks[3], (DEPTH, 2, D_MODEL, D_FF), D_MODEL),
        "ffn_w_up": nrm(ks[4], (DEPTH, 2, D_MODEL, D_FF), D_MODEL),
        "ffn_w_down": nrm(ks[5], (DEPTH, 2, D_FF, D_MODEL), D_FF),
        "ab_w_in": nrm(ks[6], (N_EVEN, D_MODEL, AB_IN), D_MODEL),
        "ab_w_out": nrm(ks[7], (N_EVEN, AB_OUT, D_MODEL), AB_OUT),
        "a_q_norm": gain(ks[8], (N_EVEN, HEAD_DIM)),
        "a_k_norm": gain(ks[9], (N_EVEN, HEAD_DIM)),
        "b_rpb": 0.1 * jax.random.normal(ks[10], (N_EVEN, B_HEADS, 2 * NA_ROWS - 1, 2 * NA_COLS - 1), f32),
        "cd_w_in": nrm(ks[11], (N_ODD, D_MODEL, CD_IN), D_MODEL),
        "cd_w_out": nrm(ks[12], (N_ODD, CD_OUT, D_MODEL), CD_OUT),
        "c_conv_w": nrm(ks[13], (N_ODD, C_CONV, C_WIDTH), C_CONV),
        "d_conv_w": nrm(ks[14], (N_ODD, D_CONV, D_WIDTH), D_CONV),
        "d_norm_g": gain(ks[15], (N_ODD, D_WIDTH)),
        "d_norm_b": 0.02 * jax.random.normal(ks[16], (N_ODD, D_WIDTH), f32),
        "final_norm": gain(ks[17], (D_MODEL,)),
    }


def reference(x, ffn_norm, mix_norm, ffn_w_gate, ffn_w_up, ffn_w_down, ab_w_in, ab_w_out,
              a_q_norm, a_k_norm, b_rpb, cd_w_in, cd_w_out, c_conv_w, d_conv_w,
              d_norm_g, d_norm_b, final_norm):
    cos, sin = axial_rope(x.shape[1])
    for i in range(DEPTH):
        x = x + 0.5 * swiglu(rms_norm(x, ffn_norm[i, 0]), ffn_w_gate[i, 0], ffn_w_up[i, 0], ffn_w_down[i, 0])
        hx = rms_norm(x, mix_norm[i])
        j = i // 2
        if i % 2 == 0:
            x = x + mixer_ab(hx, ab_w_in[j], ab_w_out[j], a_q_norm[j], a_k_norm[j], b_rpb[j], cos, sin)
        else:
            x = x + mixer_cd(hx, cd_w_in[j], cd_w_out[j], c_conv_w[j], d_conv_w[j], d_norm_g[j], d_norm_b[j])
        x = x + 0.5 * swiglu(rms_norm(x, ffn_norm[i, 1]), ffn_w_gate[i, 1], ffn_w_up[i, 1], ffn_w_down[i, 1])
    return rms_norm(x, final_norm)
```

```python
import numpy as np
import concourse.bass as bass
import concourse.mybir as mybir
from concourse.bass_utils import run_bass_kernel_spmd

F32 = mybir.dt.float32
BF16 = mybir.dt.bfloat16
AF = mybir.ActivationFunctionType
ALU = mybir.AluOpType

D = 1024
S = 2048
DFF = 2816
NF = 22
DEPTH = 4
EPS = 1e-6
TB = 512
NB = S // TB
NCORES = 8
NEGB = -30000.0
SLOT = 1408
NSLOT = 5
PRE_LOOK = 80

def _layout():
    units = {}
    off = 0

    def add(name, n):
        nonlocal off
        units[name] = (off, n)
        off += n

    def add_ffn(i, w):
        for f in range(NF):
            add(("G", i, w, f), 1024)
            add(("U", i, w, f), 1024)
        for d in range(8):
            for hh in range(2):
                add(("D", i, w, d, hh), 1408)

    for i in range(DEPTH):
        add_ffn(i, 0)
        j = i // 2
        if i % 2 == 0:
            for nm in ["ak", "bk0", "bk1", "bk2", "bk3", "av", "bv0", "bv1", "bv2", "bv3",
                       "aq0", "aq1", "aq2", "aq3", "bq0", "bq1", "bq2", "bq3"]:
                add(("AB", j, nm), 1024)
            for d in range(8):
                add(("ABO", j, d), 1024)
        else:
            for u in range(20):
                add(("CD", j, u), 1024)
            for d in range(8):
                add(("CDO", j, d), 1024)
        add_ffn(i, 1)
    return units, off


UNITS, NCW = _layout()
CVT = 512
NCW_PAD = ((NCW + CVT - 1) // CVT) * CVT

PV = {}
_o = 0
for _i in range(DEPTH):
    for _w in range(2):
        PV[("ffn", _i, _w)] = _o; _o += 8
    PV[("mix", _i)] = _o; _o += 8
PV["final"] = _o; _o += 8
for _j in range(2):
    PV[("qn", _j)] = _o; _o += 1
    PV[("kn", _j)] = _o; _o += 1
    PV[("dg", _j)] = _o; _o += 4
    PV[("db", _j)] = _o; _o += 4
    PV[("c3", _j)] = _o; _o += 12
    PV[("c31", _j)] = _o; _o += 124
NPV = _o
CM_ID = 0; CM_ROT = 128; CM_MEAN = 256; CM_ONES = 384; CM_BD = 512
NCM = 640


def _na_kts(m):
    def rs(r):
        return min(max(r - 4, 0), 24)
    lo = rs(2 * m) // 2
    hi = (rs(2 * m + 1) + 7) // 2
    return list(range(lo, hi + 1))


NA_BIDX = {}
for _m in range(16):
    NA_BIDX[_m] = {0: 1, 1: 2, 14: 3, 15: 4}.get(_m, 0)


def _pack_weights(inp):
    wall = np.zeros((128, NCW_PAD), np.float32)

    def put(name, blk):
        o, n = UNITS[name]
        p = blk.shape[0]
        wall[:p, o:o + blk.shape[1]] = blk

    for i in range(DEPTH):
        for w in range(2):
            g = inp["ffn_w_gate"][i, w].reshape(8, 128, NF, 128)
            u = inp["ffn_w_up"][i, w].reshape(8, 128, NF, 128)
            g = np.ascontiguousarray(g.transpose(2, 1, 0, 3)).reshape(NF, 128, 1024)
            u = np.ascontiguousarray(u.transpose(2, 1, 0, 3)).reshape(NF, 128, 1024)
            dn = inp["ffn_w_down"][i, w].reshape(NF, 128, 8, 128)
            dn = np.ascontiguousarray(dn.transpose(2, 1, 0, 3))
            for f in range(NF):
                put(("G", i, w, f), g[f])
                put(("U", i, w, f), u[f])
            for d in range(8):
                for hh in range(2):
                    put(("D", i, w, d, hh), dn[d][:, hh * 11:(hh + 1) * 11, :].reshape(128, 1408))
        j = i // 2
        if i % 2 == 0:
            wi = inp["ab_w_in"][j]

            def unit_cols(cols):
                b = wi[:, cols].reshape(8, 128, 128)
                return np.ascontiguousarray(b.transpose(1, 0, 2)).reshape(128, 1024)

            put(("AB", j, "ak"), unit_cols(np.arange(512, 640)))
            put(("AB", j, "av"), unit_cols(np.arange(640, 768)))
            for c in range(4):
                put(("AB", j, "bk%d" % c), unit_cols(np.arange(1280 + 128 * c, 1280 + 128 * (c + 1))))
                put(("AB", j, "bv%d" % c), unit_cols(np.arange(1792 + 128 * c, 1792 + 128 * (c + 1))))
                put(("AB", j, "bq%d" % c), unit_cols(np.arange(768 + 128 * c, 768 + 128 * (c + 1))))
                cols = np.concatenate([np.arange(64 * c, 64 * c + 64), np.arange(256 + 64 * c, 256 + 64 * c + 64)])
                put(("AB", j, "aq%d" % c), unit_cols(cols))
            wo = inp["ab_w_out"][j]
            rows = []
            for pr in range(8):
                if pr < 4:
                    rows.append(np.concatenate([np.arange(pr * 64, pr * 64 + 64), np.arange((4 + pr) * 64, (4 + pr) * 64 + 64)]))
                else:
                    rows.append(np.arange(512 + (pr - 4) * 128, 512 + (pr - 4) * 128 + 128))
            rows = np.stack(rows)
            wsel = wo[rows].reshape(8, 128, 8, 128)
            wsel = np.ascontiguousarray(wsel.transpose(2, 1, 0, 3))
            for d in range(8):
                put(("ABO", j, d), wsel[d].reshape(128, 1024))
        else:
            wi = inp["cd_w_in"][j]
            for u in range(20):
                b = wi[:, 128 * u:128 * (u + 1)].reshape(8, 128, 128)
                put(("CD", j, u), np.ascontiguousarray(b.transpose(1, 0, 2)).reshape(128, 1024))
            wo = inp["cd_w_out"][j].reshape(8, 128, 8, 128)
            wo = np.ascontiguousarray(wo.transpose(2, 1, 0, 3))
            for d in range(8):
                put(("CDO", j, d), wo[d].reshape(128, 1024))
    return wall


def _pack_pvec(inp):
    pv = np.zeros((128, NPV), np.float32)
    for i in range(DEPTH):
        for w in range(2):
            pv[:, PV[("ffn", i, w)]:PV[("ffn", i, w)] + 8] = inp["ffn_norm"][i, w].reshape(8, 128).T
        pv[:, PV[("mix", i)]:PV[("mix", i)] + 8] = inp["mix_norm"][i].reshape(8, 128).T
    pv[:, PV["final"]:PV["final"] + 8] = inp["final_norm"].reshape(8, 128).T
    for j in range(2):
        pv[:, PV[("qn", j)]] = np.tile(inp["a_q_norm"][j], 2)
        pv[:, PV[("kn", j)]] = np.tile(inp["a_k_norm"][j], 2)
        pv[:, PV[("dg", j)]:PV[("dg", j)] + 4] = inp["d_norm_g"][j].reshape(4, 128).T
        pv[:, PV[("db", j)]:PV[("db", j)] + 4] = inp["d_norm_b"][j].reshape(4, 128).T
        c3 = inp["c_conv_w"][j].reshape(3, 4, 128)
        pv[:, PV[("c3", j)]:PV[("c3", j)] + 12] = c3.transpose(2, 1, 0).reshape(128, 12)
        c31 = inp["d_conv_w"][j].reshape(31, 4, 128)
        pv[:, PV[("c31", j)]:PV[("c31", j)] + 124] = c31.transpose(2, 1, 0).reshape(128, 124)
    return pv


def _const_mats():
    cm = np.zeros((128, NCM), np.float32)
    cm[:, CM_ID:CM_ID + 128] = np.eye(128, dtype=np.float32)
    rot = np.zeros((128, 128), np.float32)
    for i in range(64):
        rot[2 * i + 1, 2 * i] = -1.0
        rot[2 * i, 2 * i + 1] = 1.0
    cm[:, CM_ROT:CM_ROT + 128] = rot
    cm[:, CM_MEAN:CM_MEAN + 128] = 1.0 / 512.0
    cm[:, CM_ONES:CM_ONES + 128] = 1.0
    bd = np.zeros((128, 128), np.float32)
    bd[:64, :64] = 1.0
    bd[64:, 64:] = 1.0
    cm[:, CM_BD:CM_BD + 128] = bd
    return cm


def _rope_tables():
    t = np.arange(S)
    row = (t // 64).astype(np.float32)
    col = (t % 64).astype(np.float32)
    half = 32
    freqs = (np.float32(10000.0) ** (-np.arange(0, half, 2, dtype=np.float32) / np.float32(half))).astype(np.float32)
    ang = np.concatenate([row[:, None] * freqs, col[:, None] * freqs], axis=-1).astype(np.float32)
    cos = np.cos(ang).astype(np.float32)
    sin = np.sin(ang).astype(np.float32)
    idx = (np.arange(128) % 64) // 2
    tab = np.zeros((2, 128, S), np.float32)
    tab[0] = cos[:, idx].T
    tab[1] = sin[:, idx].T
    return tab


def _na_bias(rpb):
    out = np.full((2, 8, 5, 128, 5, 128), NEGB, np.float32)
    reps = {0: 4, 1: 0, 2: 1, 3: 14, 4: 15}
    for g, m in reps.items():
        kts = _na_kts(m)
        qt = np.arange(128 * m, 128 * m + 128)
        qr = qt // 64
        qc = qt % 64
        rs = np.clip(qr - 4, 0, 24)
        wcs = np.clip(qc - 8, 0, 48)
        for i, kt in enumerate(kts):
            k = np.arange(128 * kt, 128 * kt + 128)
            kr = (k // 64)[:, None]
            kc = (k % 64)[:, None]
            valid = (kr >= rs[None]) & (kr < rs[None] + 8) & (kc >= wcs[None]) & (kc < wcs[None] + 16)
            ro = np.clip(kr - qr[None] + 7, 0, 14)
            co = np.clip(kc - qc[None], -15, 15) + 15
            vals = rpb[:, :, ro, co]
            out[:, :, g, :, i, :] = np.where(valid[None, None], vals, np.float32(NEGB))
    return out


ENGS = ["pe", "act", "dve", "pool", "sp"]


class Tile:
    __slots__ = ("name", "w", "r", "rd", "const")

    def __init__(self, name, const=False):
        self.name = name
        self.w = None
        self.r = {}
        self.rd = []
        self.const = const


class DSem:
    __slots__ = ("h", "count", "idx")

    def __init__(self, idx):
        self.idx = idx
        self.h = None
        self.count = 0


class Ins:
    __slots__ = ("eng", "fn", "deps", "idx", "is_dma", "dsem", "val", "need_inc")


class Prog:
    def __init__(self):
        self.streams = {e: [] for e in ENGS}
        self.pending_barrier = {e: [] for e in ENGS}
        self.dsems = []
        self.open_dmas = {}

    def new_dsem(self):
        d = DSem(len(self.dsems))
        self.dsems.append(d)
        return d

    def _mk(self, eng, fn, reads, writes, is_dma, dsem, extra=(), arena=True):
        ins = Ins()
        ins.eng = eng
        ins.fn = fn
        ins.is_dma = is_dma
        ins.dsem = dsem
        ins.need_inc = False
        ins.val = 0
        deps = []
        for t in reads:
            if t.w is not None:
                deps.append(t.w)
        for t in writes:
            if t.w is not None:
                deps.append(t.w)
            deps.extend(t.r.values())
            deps.extend(t.rd)
        deps.extend(extra)
        if self.pending_barrier[eng] and arena:
            deps.extend(self.pending_barrier[eng])
            self.pending_barrier[eng] = []
        ins.deps = deps
        ins.idx = len(self.streams[eng])
        self.streams[eng].append(ins)
        if is_dma:
            dsem.count += 16
            ins.val = dsem.count
            if arena:
                self.open_dmas[dsem.idx] = ins
        for t in reads:
            if t.const:
                continue
            if is_dma:
                t.rd.append(ins)
            else:
                t.r[eng] = ins
        for t in writes:
            t.w = ins
            t.r = {}
            t.rd = []
        return ins

    def op(self, eng, fn, reads=(), writes=()):
        return self._mk(eng, fn, reads, writes, False, None)

    def dma(self, out, in_, dsem, reads=(), writes=(), eng="sp", extra=(), arena=True):
        return self._mk(eng, (lambda e, o=out, i=in_: e.dma_start(out=o, in_=i)), reads, writes, True, dsem, extra, arena)

    def barrier(self):
        lasts = [self.streams[e][-1] for e in ["pe", "act", "dve"] if self.streams[e]]
        lasts += list(self.open_dmas.values())
        for e in ["pe", "act", "dve", "sp"]:
            self.pending_barrier[e] = list(lasts)

    def emit(self, nc, block, sems, end_fn):
        plans = {}
        for e in ENGS:
            waited = {}
            plan = []
            for ins in self.streams[e]:
                ws = []
                for d in ins.deps:
                    if d.is_dma:
                        key = ("d", d.dsem.idx)
                        v = d.val
                    else:
                        if d.eng == e and e == "pe":
                            continue
                        if d.eng == e and d.idx >= ins.idx:
                            continue
                        key = ("e", d.eng)
                        v = d.idx
                    if waited.get(key, -1) >= v:
                        continue
                    waited[key] = v
                    ws.append(d)
                    if not d.is_dma:
                        d.need_inc = True
                plan.append(ws)
            plans[e] = plan
        for e in ENGS:
            c = 0
            for ins in self.streams[e]:
                if ins.is_dma:
                    continue
                if ins.need_inc:
                    c += 1
                    ins.val = c
        engmap = {"pe": block.tensor, "act": block.scalar, "dve": block.vector, "pool": block.gpsimd, "sp": block.sync}

        def make_body(e):
            def body(eng):
                for ins, ws in zip(self.streams[e], plans[e]):
                    best = {}
                    for d in ws:
                        if d.is_dma:
                            k = ("d", d.dsem.idx)
                            h = d.dsem.h
                        else:
                            k = ("e", d.eng)
                            h = sems[d.eng]
                        if k not in best or best[k][1] < d.val:
                            best[k] = (h, d.val)
                    for h, v in best.values():
                        eng.wait_ge(h, v)
                    r = ins.fn(eng)
                    if ins.is_dma:
                        r.then_inc(ins.dsem.h, 16)
                    elif ins.need_inc:
                        r.then_inc(sems[e], 1)
                if end_fn is not None:
                    end_fn(e, eng)
            return body

        for e in ENGS:
            engmap[e](make_body(e))


class Ring:
    def __init__(self, prog, ring_ap_fn, wbf):
        self.prog = prog
        self.ap_fn = ring_ap_fn
        self.wbf = wbf
        self.tiles = [Tile("ring%d" % i) for i in range(NSLOT)]
        self.dsems = [prog.new_dsem() for _ in range(NSLOT)]
        self.seq = None
        self.pump_to = None
        self.pumped_unit = -1
        self.cvt_out = {}
        self.rec = []
        self.pos = 0
        self.issued = 0

    def _issue(self, k):
        name, parts = self.seq[k]
        off, n = UNITS[name]
        s = k % NSLOT
        extra = ()
        if self.cvt_out:
            cmax = (off + n - 1) // CVT
            extra = tuple(self.cvt_out[c] for c in (cmax, cmax - 1) if c in self.cvt_out)
        self.prog.dma(self.ap_fn(s)[0:parts, 0:n], self.wbf[0:parts, off:off + n], self.dsems[s],
                      reads=(), writes=(self.tiles[s],), extra=extra, arena=False)

    def get(self, name, parts=128):
        if self.seq is None:
            self.rec.append((name, parts))
            k = len(self.rec) - 1
            return self.tiles[k % NSLOT], self.ap_fn(k % NSLOT)
        k = self.pos
        assert self.seq[k] == (name, parts), (self.seq[k], name)
        if self.pump_to is not None:
            ku = min(len(self.seq) - 1, k + PRE_LOOK)
            if ku > self.pumped_unit:
                self.pumped_unit = ku
                o, n = UNITS[self.seq[ku][0]]
                self.pump_to((o + n - 1) // CVT)
        while self.issued < min(len(self.seq), k + NSLOT - 1):
            self._issue(self.issued)
            self.issued += 1
        self.pos += 1
        return self.tiles[k % NSLOT], self.ap_fn(k % NSLOT)


def _emit_program(P, T, cfg, recorded_seq):
    nseq = cfg["nseq"]
    stages = cfg["stages"]
    xT = T["xT"]
    arena = T["arena"]
    pvec = T["pvec"]
    cmat = T["cmat"]
    cbf = T["cbf"]
    ps = T["ps"]
    AW = T["AW"]

    consts = Tile("consts", const=True)
    pst = [Tile("ps%d" % i) for i in range(8)]
    psn = [0]

    def psum():
        i = 2 + psn[0] % 6
        psn[0] += 1
        return pst[i], ps[i]

    pan = [0]

    def psum_acc():
        i = pan[0] % 2
        pan[0] += 1
        return pst[i], ps[i]

    xt_tiles = [[Tile("x%d_%d" % (c, b)) for b in range(NB)] for c in range(8)]

    ring = Ring(P, lambda s: T["ring"][:, s * SLOT:(s + 1) * SLOT], T["wbf"])
    ring.seq = recorded_seq

    class Carver:
        def __init__(self):
            self.off = 0

        def f32(self, n):
            a = arena[:, self.off:self.off + n]
            self.off += n
            assert self.off <= AW, (self.off, AW)
            return a

        def bf(self, n):
            w = (n + 1) // 2
            a = arena[:, self.off:self.off + w].bitcast(BF16)
            self.off += w
            assert self.off <= AW, (self.off, AW)
            return a

    ID = cmat[:, CM_ID:CM_ID + 128]
    ROT = cmat[:, CM_ROT:CM_ROT + 128]
    MEANM = cmat[:, CM_MEAN:CM_MEAN + 128]
    ONESB = cbf[:, 0:128]
    BDB = cbf[:, 128:256]
    IDB = cbf[:, 256:384]

    cds = P.new_dsem()
    P.dma(pvec, T["pvec_d"], cds, writes=(consts,))
    P.dma(cmat, T["cmat_d"], cds, writes=(consts,))
    P.op("dve", lambda e: e.tensor_copy(out=cbf[:, 0:128], in_=cmat[:, CM_ONES:CM_ONES + 128]), reads=(consts,), writes=(consts,))
    P.op("dve", lambda e: e.tensor_copy(out=cbf[:, 128:256], in_=cmat[:, CM_BD:CM_BD + 128]), reads=(consts,), writes=(consts,))
    P.op("dve", lambda e: e.tensor_copy(out=cbf[:, 256:384], in_=cmat[:, CM_ID:CM_ID + 128]), reads=(consts,), writes=(consts,))

    if cfg.get("prepass", True) and recorded_seq is not None:
        NBUF = 3
        stin = T["cvt_in"]
        stout = T["cvt_out"]
        st_in = [stin[:, b * CVT:(b + 1) * CVT] for b in range(NBUF)]
        st_out = [stout[:, b * CVT:(b + 1) * CVT] for b in range(NBUF)]
        t_in = [Tile("cvi%d" % i) for i in range(NBUF)]
        t_out = [Tile("cvo%d" % i) for i in range(NBUF)]
        ds_in = [P.new_dsem() for _ in range(NBUF)]
        ds_out = [P.new_dsem() for _ in range(NBUF)]
        klo, khi = cfg.get("prepass_range", (0, NCW_PAD // CVT))
        pst_ = {"next_in": klo, "next_out": klo}

        def emit_out(k):
            b = k % NBUF
            ring.cvt_out[k] = P.dma(T["wbf"][:, k * CVT:(k + 1) * CVT], st_out[b], ds_out[b], reads=(t_out[b],), arena=False)
            pst_["next_out"] = k + 1

        def pump_to(kneed):
            kneed = min(kneed, khi - 1)
            while pst_["next_out"] <= kneed:
                k = pst_["next_in"]
                if k < khi:
                    b = k % NBUF
                    P.dma(st_in[b], T["wall"][:, k * CVT:(k + 1) * CVT], ds_in[b], writes=(t_in[b],), arena=False)
                    P.op("pool", lambda e, b=b: e.tensor_copy(out=st_out[b], in_=st_in[b]), reads=(t_in[b],), writes=(t_out[b],))
                    pst_["next_in"] = k + 1
                if pst_["next_out"] <= pst_["next_in"] - 3 or pst_["next_in"] >= khi:
                    emit_out(pst_["next_out"])

        ring.pump_to = pump_to

    def rms_stats(cv_tmp, blk, out_rstd, rstd_tile):
        sq, sqt = cv_tmp
        pt, pa = psum()
        for c in range(8):
            i = c % 2
            P.op("act", lambda e, c=c, i=i: e.activation(out=sq[i], in_=xT[:, c, blk * TB:(blk + 1) * TB], func=AF.Square),
                 reads=(xt_tiles[c][blk],), writes=(sqt[i],))
            P.op("pe", lambda e, c=c, i=i: e.matmul(pa, ONESB, sq[i], start=(c == 0), stop=(c == 7)),
                 reads=(sqt[i], consts), writes=(pt,))
        P.op("act", lambda e: e.activation(out=out_rstd, in_=pa, func=AF.Ln, bias=EPS, scale=1.0 / D),
             reads=(pt,), writes=(rstd_tile,))
        P.op("act", lambda e: e.activation(out=out_rstd, in_=out_rstd, func=AF.Exp, scale=-0.5),
             reads=(rstd_tile,), writes=(rstd_tile,))

    def apply_norm(blk, gcol, rstd, rstd_tile, out_fn, out_tiles):
        for c in range(8):
            P.op("dve", lambda e, c=c: e.scalar_tensor_tensor(out=out_fn(c), in0=xT[:, c, blk * TB:(blk + 1) * TB],
                                                              scalar=pvec[:, gcol + c:gcol + c + 1], in1=rstd,
                                                              op0=ALU.mult, op1=ALU.mult),
                 reads=(xt_tiles[c][blk], rstd_tile, consts), writes=(out_tiles[c],))

    def load_x(s):
        cv = Carver()
        stg = [cv.f32(1024) for _ in range(2)]
        stt = [Tile("ldst%d" % i) for i in range(2)]
        sds = [P.new_dsem() for _ in range(2)]
        for tt in range(16):
            b = tt % 2
            P.dma(stg[b], T["x_d"][s * S + tt * 128:s * S + (tt + 1) * 128, :], sds[b], writes=(stt[b],))
            for half in range(2):
                pt, pa = psum()
                for q in range(4):
                    c = half * 4 + q
                    P.op("pe", lambda e, c=c, q=q, b=b, pa=pa: e.transpose(pa[:, q * 128:(q + 1) * 128], stg[b][:, c * 128:(c + 1) * 128], ID),
                         reads=(stt[b], consts), writes=(pt,))
                eg = "act" if half == 0 else "dve"
                dst = xT[:, half * 4:half * 4 + 4, tt * 128:(tt + 1) * 128]
                src = pa.rearrange("p (q t) -> p q t", q=4)
                wr = tuple(xt_tiles[half * 4 + q][tt // 4] for q in range(4))
                if eg == "act":
                    P.op("act", lambda e, dst=dst, src=src: e.copy(out=dst, in_=src), reads=(pt,), writes=wr)
                else:
                    P.op("dve", lambda e, dst=dst, src=src: e.tensor_copy(out=dst, in_=src), reads=(pt,), writes=wr)
        P.barrier()

    def store_out(s):
        cv = Carver()
        sq = [cv.bf(TB) for _ in range(2)]
        sqt = [Tile("fsq%d" % i) for i in range(2)]
        rstd = cv.f32(TB)
        rstd_t = Tile("frstd")
        xn = cv.f32(8 * TB).rearrange("p (c t) -> p c t", c=8)
        xn_t = [Tile("fxn%d" % c) for c in range(8)]
        stg = [cv.f32(1024) for _ in range(2)]
        stt = [Tile("fst%d" % i) for i in range(2)]
        ods = [P.new_dsem() for _ in range(2)]
        k = 0
        for blk in range(NB):
            rms_stats((sq, sqt), blk, rstd, rstd_t)
            apply_norm(blk, PV["final"], rstd, rstd_t, lambda c: xn[:, c, :], xn_t)
            for t4 in range(4):
                b = k % 2
                k += 1
                for half in range(2):
                    pt, pa = psum()
                    for q in range(4):
                        c = half * 4 + q
                        P.op("pe", lambda e, c=c, q=q, pa=pa, t4=t4: e.transpose(pa[:, q * 128:(q + 1) * 128], xn[:, c, t4 * 128:(t4 + 1) * 128], ID),
                             reads=(xn_t[c], consts), writes=(pt,))
                    dst = stg[b][:, half * 512:(half + 1) * 512]
                    if half == 0:
                        P.op("act", lambda e, dst=dst, pa=pa: e.copy(out=dst, in_=pa), reads=(pt,), writes=(stt[b],))
                    else:
                        P.op("dve", lambda e, dst=dst, pa=pa: e.tensor_copy(out=dst, in_=pa), reads=(pt,), writes=(stt[b],))
                tt = blk * 4 + t4
                P.dma(T["out_d"][s * S + tt * 128:s * S + (tt + 1) * 128, :], stg[b], ods[b], reads=(stt[b],))
        P.barrier()

    def ffn(i, w):
        cv = Carver()
        xn = cv.bf(8 * 1024).rearrange("p (c t) -> p c t", c=8)
        h = cv.bf(NF * 1024).rearrange("p (f t) -> p f t", f=NF)
        sq = [cv.bf(TB) for _ in range(2)]
        sqt = [Tile("sq%d" % k) for k in range(2)]
        rstd = [cv.f32(TB) for _ in range(2)]
        rstd_t = [Tile("rstd%d" % k) for k in range(2)]
        sg = [cv.f32(TB) for _ in range(3)]
        sg_t = [Tile("sg%d" % k) for k in range(3)]
        xn_t = [[Tile("xn%d_%d" % (c, b)) for b in range(2)] for c in range(8)]
        h_t = [[Tile("h%d_%d" % (f, b)) for b in range(2)] for f in range(NF)]
        gcol = PV[("ffn", i, w)]
        nsg = 0
        sq8 = [[cv.bf(TB) for _ in range(8)] for _ in range(2)]
        sq8t = [[Tile("sq8_%d_%d" % (b, c)) for c in range(8)] for b in range(2)]

        def norm_sq(half):
            for b2 in range(2):
                blk = half * 2 + b2
                for c in range(8):
                    P.op("act", lambda e, c=c, b2=b2, blk=blk: e.activation(out=sq8[b2][c], in_=xT[:, c, blk * TB:(blk + 1) * TB], func=AF.Square),
                         reads=(xt_tiles[c][blk],), writes=(sq8t[b2][c],))

        def norm_mm(half):
            for b2 in range(2):
                blk = half * 2 + b2
                pt, pa = psum()
                for c in range(8):
                    P.op("pe", lambda e, c=c, b2=b2, pa=pa: e.matmul(pa, ONESB, sq8[b2][c], start=(c == 0), stop=(c == 7)),
                         reads=(sq8t[b2][c], consts), writes=(pt,))
                P.op("act", lambda e, b2=b2, pa=pa: e.activation(out=rstd[b2], in_=pa, func=AF.Ln, bias=EPS, scale=1.0 / D),
                     reads=(pt,), writes=(rstd_t[b2],))
                P.op("act", lambda e, b2=b2: e.activation(out=rstd[b2], in_=rstd[b2], func=AF.Exp, scale=-0.5),
                     reads=(rstd_t[b2],), writes=(rstd_t[b2],))
                apply_norm(blk, gcol, rstd[b2], rstd_t[b2], lambda c, b2=b2: xn[:, c, b2 * TB:(b2 + 1) * TB],
                           [xn_t[c][b2] for c in range(8)])

        norm_sq(0)
        norm_mm(0)
        for half in range(2):
            for f in range(NF):
                gt, ga = ring.get(("G", i, w, f))
                ut, ua = ring.get(("U", i, w, f))
                for b2 in range(2):
                    pgt, pga = psum()
                    put, pua = psum()
                    for k in range(8):
                        P.op("pe", lambda e, k=k, b2=b2, pga=pga, ga=ga: e.matmul(pga, ga[:, k * 128:(k + 1) * 128], xn[:, k, b2 * TB:(b2 + 1) * TB], start=(k == 0), stop=(k == 7)),
                             reads=(gt, xn_t[k][b2]), writes=(pgt,))
                    for k in range(8):
                        P.op("pe", lambda e, k=k, b2=b2, pua=pua, ua=ua: e.matmul(pua, ua[:, k * 128:(k + 1) * 128], xn[:, k, b2 * TB:(b2 + 1) * TB], start=(k == 0), stop=(k == 7)),
                             reads=(ut, xn_t[k][b2]), writes=(put,))
                    si = nsg % 3
                    nsg += 1
                    P.op("act", lambda e, si=si, pga=pga: e.activation(out=sg[si], in_=pga, func=AF.Silu), reads=(pgt,), writes=(sg_t[si],))
                    P.op("dve", lambda e, si=si, pua=pua, f=f, b2=b2: e.tensor_tensor(out=h[:, f, b2 * TB:(b2 + 1) * TB], in0=pua, in1=sg[si], op=ALU.mult),
                         reads=(put, sg_t[si]), writes=(h_t[f][b2],))
            if half == 0:
                norm_sq(1)
            for d in range(8):
                if half == 0 and d == 1:
                    norm_mm(1)
                d0t, d0a = ring.get(("D", i, w, d, 0))
                d1t, d1a = ring.get(("D", i, w, d, 1))
                for b2 in range(2):
                    blk = half * 2 + b2
                    pyt, pya = psum()
                    for f in range(NF):
                        dt_, da = (d0t, d0a) if f < 11 else (d1t, d1a)
                        fi = f % 11
                        P.op("pe", lambda e, f=f, fi=fi, da=da, b2=b2, pya=pya: e.matmul(pya, da[:, fi * 128:(fi + 1) * 128], h[:, f, b2 * TB:(b2 + 1) * TB], start=(f == 0), stop=(f == NF - 1)),
                             reads=(dt_, h_t[f][b2]), writes=(pyt,))
                    xs = xT[:, d, blk * TB:(blk + 1) * TB]
                    P.op("dve", lambda e, xs=xs, pya=pya: e.scalar_tensor_tensor(out=xs, in0=pya, scalar=0.5, in1=xs, op0=ALU.mult, op1=ALU.add),
                         reads=(pyt, xt_tiles[d][blk]), writes=(xt_tiles[d][blk],))
        P.barrier()

    def mixer_cd(i):
        j = i // 2
        cv = Carver()
        PW = 2052
        GW = 2080
        pT = cv.bf(4 * PW).rearrange("p (c t) -> p c t", c=4)
        gT = cv.bf(4 * GW).rearrange("p (c t) -> p c t", c=4)
        cbT = cv.bf(4 * S).rearrange("p (c t) -> p c t", c=4)
        p_t = [[Tile("p%d_%d" % (c, b)) for b in range(NB)] for c in range(4)]
        g_t = [[Tile("g%d_%d" % (c, b)) for b in range(NB)] for c in range(4)]
        cb_t = [[Tile("cb%d_%d" % (c, b)) for b in range(NB)] for c in range(4)]
        pad_t = Tile("pads")
        dg3 = cv.bf(12 * 128).rearrange("p (n m) -> p n m", n=12)
        dg31 = cv.bf(124 * 128).rearrange("p (n m) -> p n m", n=124)
        dg_t = Tile("diag")
        mark = cv.off
        hxb = [cv.bf(8 * TB).rearrange("p (c t) -> p c t", c=8) for _ in range(2)]
        hx_t = [[Tile("hx%d_%d" % (b, c)) for c in range(8)] for b in range(2)]
        sq = [cv.bf(TB) for _ in range(2)]
        sqt = [Tile("sq%d" % k) for k in range(2)]
        rstd = cv.f32(TB)
        rstd_t = Tile("rstd")
        tm = [cv.f32(TB) for _ in range(4)]
        tm_t = [Tile("tm%d" % k) for k in range(4)]
        for c in range(4):
            P.op("dve", lambda e, c=c: e.memset(pT[:, c, 0:2], 0.0), writes=(pad_t,))
            P.op("dve", lambda e, c=c: e.memset(pT[:, c, 2 + S:PW], 0.0), writes=(pad_t,))
            P.op("dve", lambda e, c=c: e.memset(gT[:, c, 0:16], 0.0), writes=(pad_t,))
            P.op("dve", lambda e, c=c: e.memset(gT[:, c, 16 + S:GW], 0.0), writes=(pad_t,))
        for c in range(4):
            for k in range(3):
                col = PV[("c3", j)] + c * 3 + k
                P.op("dve", lambda e, c=c, k=k, col=col: e.tensor_scalar(out=dg3[:, c * 3 + k, :], in0=IDB, scalar1=pvec[:, col:col + 1], scalar2=None, op0=ALU.mult),
                     reads=(consts,), writes=(dg_t,))
            for k in range(31):
                col = PV[("c31", j)] + c * 31 + k
                eg = "dve"
                P.op(eg, lambda e, c=c, k=k, col=col: e.tensor_scalar(out=dg31[:, c * 31 + k, :], in0=IDB, scalar1=pvec[:, col:col + 1], scalar2=None, op0=ALU.mult),
                     reads=(consts,), writes=(dg_t,))
        ntm = 0
        for blk in range(NB):
            hb = blk % 2
            rms_stats((sq, sqt), blk, rstd, rstd_t)
            apply_norm(blk, PV[("mix", i)], rstd, rstd_t, lambda c, hb=hb: hxb[hb][:, c, :], hx_t[hb])

            def proj(u, hb=hb):
                wt, wa = ring.get(("CD", j, u))
                pt, pa = psum()
                for k in range(8):
                    P.op("pe", lambda e, k=k, wa=wa, pa=pa, hb=hb: e.matmul(pa, wa[:, k * 128:(k + 1) * 128], hxb[hb][:, k, :], start=(k == 0), stop=(k == 7)),
                         reads=(wt, hx_t[hb][k]), writes=(pt,))
                return pt, pa

            for c in range(4):
                pt1, pa1 = proj(0 + c)
                ti = ntm % 4; ntm += 1
                P.op("act", lambda e, ti=ti, pa1=pa1: e.copy(out=tm[ti], in_=pa1), reads=(pt1,), writes=(tm_t[ti],))
                pt2, pa2 = proj(8 + c)
                P.op("dve", lambda e, ti=ti, pa2=pa2, c=c, blk=blk: e.tensor_tensor(out=pT[:, c, 2 + blk * TB:2 + (blk + 1) * TB], in0=pa2, in1=tm[ti], op=ALU.mult),
                     reads=(pt2, tm_t[ti]), writes=(p_t[c][blk],))
                pt3, pa3 = proj(4 + c)
                P.op("act", lambda e, pa3=pa3, c=c, blk=blk: e.copy(out=cbT[:, c, blk * TB:(blk + 1) * TB], in_=pa3), reads=(pt3,), writes=(cb_t[c][blk],))
                pt4, pa4 = proj(16 + c)
                ti = ntm % 4; ntm += 1
                P.op("act", lambda e, ti=ti, pa4=pa4: e.activation(out=tm[ti], in_=pa4, func=AF.Sigmoid), reads=(pt4,), writes=(tm_t[ti],))
                pt5, pa5 = proj(12 + c)
                P.op("dve", lambda e, ti=ti, pa5=pa5, c=c, blk=blk: e.tensor_tensor(out=gT[:, c, 16 + blk * TB:16 + (blk + 1) * TB], in0=pa5, in1=tm[ti], op=ALU.mult),
                     reads=(pt5, tm_t[ti]), writes=(g_t[c][blk],))
        P.barrier()
        cv.off = mark
        yb = cv.bf(8 * TB).rearrange("p (c t) -> p c t", c=8)
        yb_t = [Tile("yb%d" % c) for c in range(8)]
        cvv = cv.f32(4 * TB).rearrange("p (c t) -> p c t", c=4)
        cv_t = [Tile("cv%d" % c) for c in range(4)]
        sqf = [cv.f32(TB) for _ in range(2)]
        sqf_t = [Tile("sqf%d" % k) for k in range(2)]
        mean = cv.f32(TB); mean_t = Tile("mean")
        m2 = cv.f32(TB); m2_t = Tile("m2")
        rs2 = cv.f32(TB); rs2_t = Tile("rs2")
        tmB = [cv.f32(TB) for _ in range(3)]
        tmB_t = [Tile("tmb%d" % k) for k in range(3)]
        ntm = 0
        for blk in range(NB):
            nb_r = [b for b in (blk - 1, blk, blk + 1) if 0 <= b < NB]
            for c in range(4):
                pt, pa = psum()
                for k in range(3):
                    P.op("pe", lambda e, c=c, k=k, pa=pa, blk=blk: e.matmul(pa, dg3[:, c * 3 + k, :], pT[:, c, blk * TB + k + 1:blk * TB + k + 1 + TB], start=(k == 0), stop=(k == 2)),
                         reads=tuple(p_t[c][b] for b in nb_r) + (dg_t, pad_t), writes=(pt,))
                P.op("dve", lambda e, c=c, pa=pa, blk=blk: e.tensor_tensor(out=yb[:, c, :], in0=pa, in1=cbT[:, c, blk * TB:(blk + 1) * TB], op=ALU.mult),
                     reads=(pt, cb_t[c][blk]), writes=(yb_t[c],))
            for c in range(4):
                pt, pa = psum()
                for k in range(31):
                    P.op("pe", lambda e, c=c, k=k, pa=pa, blk=blk: e.matmul(pa, dg31[:, c * 31 + k, :], gT[:, c, blk * TB + k + 1:blk * TB + k + 1 + TB], start=(k == 0), stop=(k == 30)),
                         reads=tuple(g_t[c][b] for b in nb_r) + (dg_t, pad_t), writes=(pt,))
                P.op("act", lambda e, c=c, pa=pa: e.copy(out=cvv[:, c, :], in_=pa), reads=(pt,), writes=(cv_t[c],))
            pmt, pma = psum()
            for c in range(4):
                P.op("pe", lambda e, c=c, pma=pma: e.matmul(pma, MEANM, cvv[:, c, :], start=(c == 0), stop=(c == 3)),
                     reads=(cv_t[c], consts), writes=(pmt,))
            pqt, pqa = psum()
            for c in range(4):
                si = c % 2
                P.op("act", lambda e, c=c, si=si: e.activation(out=sqf[si], in_=cvv[:, c, :], func=AF.Square), reads=(cv_t[c],), writes=(sqf_t[si],))
                P.op("pe", lambda e, c=c, si=si, pqa=pqa: e.matmul(pqa, MEANM, sqf[si], start=(c == 0), stop=(c == 3)),
                     reads=(sqf_t[si], consts), writes=(pqt,))
            P.op("act", lambda e, pma=pma: e.copy(out=mean, in_=pma), reads=(pmt,), writes=(mean_t,))
            P.op("dve", lambda e: e.tensor_tensor(out=m2, in0=mean, in1=mean, op=ALU.mult), reads=(mean_t,), writes=(m2_t,))
            P.op("dve", lambda e, pqa=pqa: e.tensor_tensor(out=rs2, in0=pqa, in1=m2, op=ALU.subtract), reads=(pqt, m2_t), writes=(rs2_t,))
            P.op("act", lambda e: e.activation(out=rs2, in_=rs2, func=AF.Ln, bias=EPS, scale=1.0), reads=(rs2_t,), writes=(rs2_t,))
            P.op("act", lambda e: e.activation(out=rs2, in_=rs2, func=AF.Exp, scale=-0.5), reads=(rs2_t,), writes=(rs2_t,))
            for c in range(4):
                ti = ntm % 3; ntm += 1
                P.op("dve", lambda e, c=c, ti=ti: e.tensor_tensor(out=tmB[ti], in0=cvv[:, c, :], in1=mean, op=ALU.subtract),
                     reads=(cv_t[c], mean_t), writes=(tmB_t[ti],))
                P.op("dve", lambda e, ti=ti: e.tensor_tensor(out=tmB[ti], in0=tmB[ti], in1=rs2, op=ALU.mult),
                     reads=(tmB_t[ti], rs2_t), writes=(tmB_t[ti],))
                gc = PV[("dg", j)] + c
                bc = PV[("db", j)] + c
                P.op("act", lambda e, c=c, ti=ti, gc=gc, bc=bc: e.activation(out=yb[:, 4 + c, :], in_=tmB[ti], func=AF.Silu, bias=pvec[:, bc:bc + 1], scale=pvec[:, gc:gc + 1]),
                     reads=(tmB_t[ti], consts), writes=(yb_t[4 + c],))
            for d in range(8):
                wt, wa = ring.get(("CDO", j, d))
                pt, pa = psum()
                for cc in range(8):
                    P.op("pe", lambda e, cc=cc, wa=wa, pa=pa: e.matmul(pa, wa[:, cc * 128:(cc + 1) * 128], yb[:, cc, :], start=(cc == 0), stop=(cc == 7)),
                         reads=(wt, yb_t[cc]), writes=(pt,))
                xs = xT[:, d, blk * TB:(blk + 1) * TB]
                P.op("dve", lambda e, xs=xs, pa=pa: e.tensor_tensor(out=xs, in0=pa, in1=xs, op=ALU.add),
                     reads=(pt, xt_tiles[d][blk]), writes=(xt_tiles[d][blk],))
        P.barrier()

    def mixer_ab(i):
        j = i // 2
        scale = 0.125
        cv = Carver()
        akT = cv.bf(S)
        bkT = cv.bf(4 * S).rearrange("p (c t) -> p c t", c=4)
        vaug2d = cv.bf(16 * 5 * 192)
        vaug = vaug2d.rearrange("p (t r e) -> p t r e", t=16, r=5)
        vaug5 = vaug2d.rearrange("p (t r k e) -> p t r k e", t=16, r=5, k=3)
        ak_t = [Tile("ak%d" % b) for b in range(NB)]
        bk_t = [[Tile("bk%d_%d" % (c, b)) for b in range(NB)] for c in range(4)]
        v_t = [Tile("v%d" % t) for t in range(16)]
        hxb = [cv.bf(8 * TB).rearrange("p (c t) -> p c t", c=8) for _ in range(1)]
        hx_t = [[Tile("hx%d_%d" % (b, c)) for c in range(8)] for b in range(2)]
        sq = [cv.bf(TB) for _ in range(2)]
        sqt = [Tile("sq%d" % k) for k in range(2)]
        rstd = cv.f32(TB); rstd_t = Tile("rstd")
        tabs = cv.f32(2 * TB).rearrange("p (a t) -> p a t", a=2)
        tab_t = Tile("tab")
        tab_ds = P.new_dsem()
        qsb = [cv.f32(TB) for _ in range(2)]; qsb_t = [Tile("qsb%d" % k) for k in range(2)]
        sqh = [cv.bf(TB) for _ in range(2)]; sqh_t = [Tile("sqh%d" % k) for k in range(2)]
        rsh = [cv.f32(TB) for _ in range(2)]; rsh_t = [Tile("rsh%d" % k) for k in range(2)]
        qn = [cv.f32(TB) for _ in range(2)]; qn_t = [Tile("qn%d" % k) for k in range(2)]
        markA = cv.off
        hxb.append(cv.bf(8 * TB).rearrange("p (c t) -> p c t", c=8))

        P.op("dve", lambda e: e.memset(vaug2d, 1.0), writes=tuple(v_t))

        def load_tabs(blk):
            P.dma(tabs, T["rope_d"][:, :, blk * TB:(blk + 1) * TB].rearrange("a p t -> p a t"), tab_ds, writes=(tab_t,))

        def rope_s1(pt, pa, k):
            P.op("act", lambda e: e.copy(out=qsb[k], in_=pa), reads=(pt,), writes=(qsb_t[k],))
            P.op("act", lambda e: e.activation(out=sqh[k], in_=pa, func=AF.Square), reads=(pt,), writes=(sqh_t[k],))

        def rope_s2(k, gcol):
            st, sa = psum()
            P.op("pe", lambda e: e.matmul(sa, BDB, sqh[k], start=True, stop=True), reads=(sqh_t[k], consts), writes=(st,))
            P.op("act", lambda e: e.activation(out=rsh[k], in_=sa, func=AF.Ln, bias=EPS, scale=1.0 / 64), reads=(st,), writes=(rsh_t[k],))
            P.op("act", lambda e: e.activation(out=rsh[k], in_=rsh[k], func=AF.Exp, scale=-0.5), reads=(rsh_t[k],), writes=(rsh_t[k],))
            P.op("dve", lambda e: e.scalar_tensor_tensor(out=qn[k], in0=qsb[k], scalar=pvec[:, gcol:gcol + 1], in1=rsh[k], op0=ALU.mult, op1=ALU.mult),
                 reads=(qsb_t[k], rsh_t[k], consts), writes=(qn_t[k],))

        def rope_s3(k, out_ap, out_tile):
            rt, ra = psum()
            P.op("pe", lambda e: e.matmul(ra, ROT, qn[k], start=True, stop=True), reads=(qn_t[k], consts), writes=(rt,))
            P.op("dve", lambda e: e.tensor_tensor(out=rsh[k], in0=qn[k], in1=tabs[:, 0, :], op=ALU.mult), reads=(qn_t[k], tab_t), writes=(rsh_t[k],))
            P.op("dve", lambda e: e.tensor_tensor(out=qsb[k], in0=ra, in1=tabs[:, 1, :], op=ALU.mult), reads=(rt, tab_t), writes=(qsb_t[k],))
            P.op("dve", lambda e: e.tensor_tensor(out=out_ap, in0=rsh[k], in1=qsb[k], op=ALU.add), reads=(rsh_t[k], qsb_t[k]), writes=(out_tile,))

        def proj_fm(name, hb):
            wt, wa = ring.get(("AB", j, name))
            pt, pa = psum()
            for k in range(8):
                P.op("pe", lambda e, k=k, wa=wa, pa=pa: e.matmul(pa, wa[:, k * 128:(k + 1) * 128], hxb[hb][:, k, :], start=(k == 0), stop=(k == 7)),
                     reads=(wt, hx_t[hb][k]), writes=(pt,))
            return pt, pa

        for blk in range(NB):
            hb = blk % 2
            rms_stats((sq, sqt), blk, rstd, rstd_t)
            apply_norm(blk, PV[("mix", i)], rstd, rstd_t, lambda c, hb=hb: hxb[hb][:, c, :], hx_t[hb])
            load_tabs(blk)
            pt, pa = proj_fm("ak", hb)
            rope_s1(pt, pa, 0)
            for c in range(4):
                pt, pa = proj_fm("bk%d" % c, hb)
                if c % 2 == 0:
                    P.op("act", lambda e, pa=pa, c=c, blk=blk: e.copy(out=bkT[:, c, blk * TB:(blk + 1) * TB], in_=pa), reads=(pt,), writes=(bk_t[c][blk],))
                else:
                    P.op("dve", lambda e, pa=pa, c=c, blk=blk: e.tensor_copy(out=bkT[:, c, blk * TB:(blk + 1) * TB], in_=pa), reads=(pt,), writes=(bk_t[c][blk],))
                if c == 0:
                    rope_s2(0, PV[("kn", j)])
                if c == 2:
                    rope_s3(0, akT[:, blk * TB:(blk + 1) * TB], ak_t[blk])
            for ui, name in enumerate(["av", "bv0", "bv1", "bv2", "bv3"]):
                wt, wa = ring.get(("AB", j, name))
                pt, pa = psum()
                for t4 in range(4):
                    for k in range(8):
                        P.op("pe", lambda e, k=k, wa=wa, pa=pa, t4=t4, hb=hb: e.matmul(pa[:, t4 * 128:(t4 + 1) * 128], hxb[hb][:, k, t4 * 128:(t4 + 1) * 128], wa[:, k * 128:(k + 1) * 128], start=(k == 0), stop=(k == 7)),
                             reads=(wt, hx_t[hb][k]), writes=(pt,))
                src = pa.rearrange("p (t h e) -> p t h e", t=4, h=2)
                dst = vaug5[:, blk * 4:(blk + 1) * 4, ui, 0:3:2, :]
                wr = tuple(v_t[blk * 4 + t4] for t4 in range(4))
                if ui % 2 == 0:
                    P.op("dve", lambda e, src=src, dst=dst: e.tensor_copy(out=dst, in_=src), reads=(pt,), writes=wr)
                else:
                    P.op("act", lambda e, src=src, dst=dst: e.copy(out=dst, in_=src), reads=(pt,), writes=wr)
        P.barrier()
        cv.off = markA
        qb = [cv.bf(8 * TB).rearrange("p (c t) -> p c t", c=8) for _ in range(1)]
        qb_t = [[Tile("qb%d_%d" % (b, c)) for c in range(8)] for b in range(1)]
        yb = cv.bf(8 * TB).rearrange("p (h t) -> p h t", h=8)
        yb_t = [Tile("yb%d" % h) for h in range(16)]
        NPB = 4
        pbuf = [cv.bf(640) for _ in range(NPB)]
        pb_t = [Tile("pb%d" % k) for k in range(NPB)]
        biasb = [cv.f32(640).rearrange("p (k q) -> p k q", k=5) for _ in range(2)]
        bias_t = [Tile("bias%d" % k) for k in range(2)]
        bias_ds = [P.new_dsem() for _ in range(2)]
        NNAT = 3
        nat = [cv.f32(640) for _ in range(NNAT)]
        nat_t = [Tile("nat%d" % k) for k in range(NNAT)]
        rt_ = cv.f32(TB); rt_t = Tile("rt")
        npb = [0]
        nbias = [0]
        nnat = [0]

        for blk in range(NB):
            hb = 0
            rms_stats((sq, sqt), blk, rstd, rstd_t)
            apply_norm(blk, PV[("mix", i)], rstd, rstd_t, lambda c: hxb[0][:, c, :], hx_t[0])
            load_tabs(blk)
            def bq_proj(c):
                pt, pa = proj_fm("bq%d" % c, 0)
                P.op("dve", lambda e, pa=pa, c=c: e.tensor_copy(out=qb[0][:, 4 + c, :], in_=pa), reads=(pt,), writes=(qb_t[0][4 + c],))

            def q_pair(c0):
                for k in range(2):
                    pt, pa = proj_fm("aq%d" % (c0 + k), 0)
                    rope_s1(pt, pa, k)
                for k in range(2):
                    rope_s2(k, PV[("qn", j)])
                bq_proj(c0)
                bq_proj(c0 + 1)
                for k in range(2):
                    rope_s3(k, qb[0][:, c0 + k, :], qb_t[0][c0 + k])

            q_pair(0)

            def finish(ot, oa, pr, bside):
                o0, d0 = (64, 0) if bside else (0, 64)
                if pr >= 4:
                    P.op("act", lambda e: e.activation(out=rt_[d0:d0 + 64, :], in_=oa[d0:d0 + 64, :], func=AF.Ln), reads=(ot,), writes=(rt_t,))
                    P.op("act", lambda e: e.activation(out=rt_[d0:d0 + 64, :], in_=rt_[d0:d0 + 64, :], func=AF.Exp, scale=-1.0), reads=(rt_t,), writes=(rt_t,))
                else:
                    P.op("dve", lambda e: e.reciprocal(out=rt_[d0:d0 + 64, :], in_=oa[d0:d0 + 64, :]), reads=(ot,), writes=(rt_t,))
                P.op("dve", lambda e: e.tensor_tensor(out=yb[o0:o0 + 64, pr, :], in0=oa[o0:o0 + 64, :], in1=rt_[d0:d0 + 64, :], op=ALU.mult),
                     reads=(ot, rt_t), writes=(yb_t[pr * 2 + bside],))

            acc = {}

            def g_qk(hq, kt):
                c = hq % 4
                pb = 0 if hq < 4 else 64
                if kt == 0:
                    acc[("g", hq)] = psum_acc()
                st, sa = psum()
                P.op("pe", lambda e, kt=kt, pb=pb, c=c, sa=sa: e.matmul(sa, akT[pb:pb + 64, kt * 128:(kt + 1) * 128], qb[0][pb:pb + 64, c, :], start=True, stop=True),
                     reads=(ak_t[kt // 4], qb_t[0][c]), writes=(st,))
                pi = npb[0] % NPB; npb[0] += 1
                P.op("act", lambda e, pi=pi, sa=sa: e.activation(out=pbuf[pi][:, 0:TB], in_=sa, func=AF.Exp, scale=scale), reads=(st,), writes=(pb_t[pi],))
                return pi

            def g_pv(hq, kt, pi):
                ot, oa = acc[("g", hq)]
                vh = hq // 4
                P.op("pe", lambda e, kt=kt, vh=vh, pi=pi, oa=oa: e.matmul(oa, vaug[:, kt, 0, 64 * vh:64 * vh + 128], pbuf[pi][:, 0:TB], start=(kt == 0), stop=(kt == 15)),
                     reads=(v_t[kt], pb_t[pi]), writes=(ot,))
                if kt == 15:
                    finish(ot, oa, hq % 4, hq // 4)

            def n_qk(h, q4):
                c = h // 2
                pb = 64 * (h % 2)
                if q4 == 0:
                    acc[("n", h)] = psum_acc()
                m = blk * 4 + q4
                kts = _na_kts(m)
                g = NA_BIDX[m]
                need_load = (q4 == 0) or (g != NA_BIDX[m - 1])
                if need_load:
                    bi = nbias[0] % 2; nbias[0] += 1
                    P.dma(biasb[bi], T["bias_d"][j, h, g], bias_ds[bi], writes=(bias_t[bi],))
                bi = (nbias[0] - 1) % 2
                nk = len(kts)
                st, sa = psum()
                st2, sa2 = (None, None)
                if nk == 5:
                    st2, sa2 = psum()
                for ki, kt in enumerate(kts):
                    tgt = sa[:, ki * 128:(ki + 1) * 128] if ki < 4 else sa2[:, 0:128]
                    tt_ = st if ki < 4 else st2
                    P.op("pe", lambda e, kt=kt, pb=pb, c=c, tgt=tgt, q4=q4: e.matmul(tgt, bkT[pb:pb + 64, c, kt * 128:(kt + 1) * 128], qb[0][pb:pb + 64, 4 + c, q4 * 128:(q4 + 1) * 128], start=True, stop=True),
                         reads=(bk_t[c][kt // 4], qb_t[0][4 + c]), writes=(tt_,))
                ni = nnat[0] % NNAT; nnat[0] += 1
                n4 = min(nk, 4) * 128
                P.op("dve", lambda e, ni=ni, sa=sa, bi=bi, n4=n4: e.scalar_tensor_tensor(out=nat[ni][:, 0:n4], in0=sa[:, 0:n4], scalar=scale, in1=biasb[bi].rearrange("p k q -> p (k q)")[:, 0:n4], op0=ALU.mult, op1=ALU.add),
                     reads=(st, bias_t[bi]), writes=(nat_t[ni],))
                if nk == 5:
                    P.op("dve", lambda e, ni=ni, sa2=sa2, bi=bi: e.scalar_tensor_tensor(out=nat[ni][:, 512:640], in0=sa2[:, 0:128], scalar=scale, in1=biasb[bi][:, 4, :], op0=ALU.mult, op1=ALU.add),
                         reads=(st2, bias_t[bi]), writes=(nat_t[ni],))
                pi = npb[0] % NPB; npb[0] += 1
                P.op("act", lambda e, pi=pi, ni=ni, nk=nk: e.activation(out=pbuf[pi][:, 0:nk * 128], in_=nat[ni][:, 0:nk * 128], func=AF.Exp), reads=(nat_t[ni],), writes=(pb_t[pi],))
                return pi

            def n_pv(h, q4, pi):
                ot, oa = acc[("n", h)]
                kts = _na_kts(blk * 4 + q4)
                nk = len(kts)
                for ki, kt in enumerate(kts):
                    P.op("pe", lambda e, kt=kt, ki=ki, h=h, pi=pi, oa=oa, q4=q4, nk=nk: e.matmul(oa[:, q4 * 128:(q4 + 1) * 128], vaug[:, kt, 1 + h // 2, 64 * (h % 2):64 * (h % 2) + 128], pbuf[pi][:, ki * 128:(ki + 1) * 128], start=(ki == 0), stop=(ki == nk - 1)),
                         reads=(v_t[kt], pb_t[pi]), writes=(ot,))
                if q4 == 3:
                    finish(ot, oa, 4 + h // 2, h % 2)

            items = [("g", hq, kt) for hq in (0, 4, 1, 5, 2, 6, 3, 7) for kt in range(16)] + [("n", h, q4) for h in range(8) for q4 in range(4)]
            pend = []
            for ii, it in enumerate(items):
                if ii == 16:
                    q_pair(2)
                pi = g_qk(it[1], it[2]) if it[0] == "g" else n_qk(it[1], it[2])
                pend.append((it, pi))
                LA = 3 if it[0] == "g" else 2
                if len(pend) > LA:
                    (i0, p0) = pend.pop(0)
                    (g_pv if i0[0] == "g" else n_pv)(i0[1], i0[2], p0)
            for (i0, p0) in pend:
                (g_pv if i0[0] == "g" else n_pv)(i0[1], i0[2], p0)
            for d in range(8):
                wt, wa = ring.get(("ABO", j, d))
                pt, pa = psum()
                for pr in range(8):
                    P.op("pe", lambda e, pr=pr, wa=wa, pa=pa: e.matmul(pa, wa[:, pr * 128:(pr + 1) * 128], yb[:, pr, :], start=(pr == 0), stop=(pr == 7)),
                         reads=(wt, yb_t[2 * pr], yb_t[2 * pr + 1]), writes=(pt,))
                xs = xT[:, d, blk * TB:(blk + 1) * TB]
                P.op("dve", lambda e, xs=xs, pa=pa: e.tensor_tensor(out=xs, in0=pa, in1=xs, op=ALU.add),
                     reads=(pt, xt_tiles[d][blk]), writes=(xt_tiles[d][blk],))
        P.barrier()

    for s in range(nseq):
        load_x(s)
        for st in stages:
            kind = st[0]
            if kind == "ffn":
                ffn(st[1], st[2])
            elif kind == "mix":
                if st[1] % 2 == 0:
                    mixer_ab(st[1])
                else:
                    mixer_cd(st[1])
        if not cfg.get("no_store"):
            store_out(s)
    return ring.rec


ALL_STAGES = []
for _i in range(DEPTH):
    ALL_STAGES += [("ffn", _i, 0), ("mix", _i), ("ffn", _i, 1)]


def build_nc(cfg):
    nseq = cfg["nseq"]
    nc = bass.Bass("TRN2", target_bir_lowering=False)
    T = {}
    T["x_d"] = nc.dram_tensor("x", [nseq * S, D], F32, kind="ExternalInput").ap()
    T["out_d"] = nc.dram_tensor("out", [nseq * S, D], F32, kind="ExternalOutput").ap()
    T["wall"] = nc.dram_tensor("wall", [128, NCW_PAD], F32, kind="ExternalInput").ap()
    T["pvec_d"] = nc.dram_tensor("pvec", [128, NPV], F32, kind="ExternalInput").ap()
    T["cmat_d"] = nc.dram_tensor("cmat", [128, NCM], F32, kind="ExternalInput").ap()
    T["rope_d"] = nc.dram_tensor("rope", [2, 128, S], F32, kind="ExternalInput").ap()
    T["bias_d"] = nc.dram_tensor("nabias", [2, 8, 5, 128, 640], F32, kind="ExternalInput").ap()
    T["wbf"] = nc.dram_tensor("wbf", [128, NCW_PAD], BF16, kind="Internal").ap()
    AW = cfg.get("AW", 29700)
    T["AW"] = AW
    import contextlib
    with contextlib.ExitStack() as es:
        xT_t = es.enter_context(nc.sbuf_tensor("xT", [128, 8 * S], F32))
        ring_t = es.enter_context(nc.sbuf_tensor("ring", [128, NSLOT * SLOT], BF16))
        pvec_t = es.enter_context(nc.sbuf_tensor("pvec_sb", [128, NPV], F32))
        cmat_t = es.enter_context(nc.sbuf_tensor("cmat_sb", [128, NCM], F32))
        cbf_t = es.enter_context(nc.sbuf_tensor("cbf_sb", [128, 384], BF16))
        arena_t = es.enter_context(nc.sbuf_tensor("arena", [128, AW], F32))
        cvi_t = es.enter_context(nc.sbuf_tensor("cvt_in", [128, 3 * CVT], F32))
        cvo_t = es.enter_context(nc.sbuf_tensor("cvt_out", [128, 3 * CVT], BF16))
        T["cvt_in"] = cvi_t[:, :]
        T["cvt_out"] = cvo_t[:, :]
        pss = [es.enter_context(nc.psum_tensor("ps%d" % k, [128, 512], F32)) for k in range(8)]
        T["xT"] = xT_t[:, :].rearrange("p (c t) -> p c t", c=8)
        T["ring"] = ring_t[:, :]
        T["pvec"] = pvec_t[:, :]
        T["cmat"] = cmat_t[:, :]
        T["cbf"] = cbf_t[:, :]
        T["arena"] = arena_t[:, :]
        T["ps"] = [p[:, :] for p in pss]
        dry = Prog()
        rec = _emit_program(dry, T, cfg, None)
        P = Prog()
        _emit_program(P, T, cfg, rec)
        sems = {e: es.enter_context(nc.semaphore("s_" + e)) for e in ENGS}
        for d in P.dsems:
            d.h = es.enter_context(nc.semaphore("d%d" % d.idx))
        block = es.enter_context(nc.Block())

        def end_fn(e, eng):
            if e == "sp":
                for d in P.dsems:
                    if d.count:
                        eng.wait_ge(d.h, d.count)

        P.emit(nc, block, sems, end_fn)
    return nc


def _prep_shared(inputs):
    inp = {k: np.asarray(v) for k, v in inputs.items()}
    wall = _pack_weights(inp)
    pv = _pack_pvec(inp)
    cm = _const_mats()
    rope = _rope_tables()
    nab = _na_bias(inp["b_rpb"].astype(np.float32)).reshape(2, 8, 5, 128, 640)
    return {"wall": wall, "pvec": pv, "cmat": cm, "rope": rope, "nabias": nab}


def kernel(**inputs):
    x = np.asarray(inputs["x"], dtype=np.float32)
    B = x.shape[0]
    per = B // NCORES
    shared = _prep_shared(inputs)
    cfg = {"nseq": per, "stages": ALL_STAGES}
    nc = build_nc(cfg)
    in_maps = []
    for c in range(NCORES):
        m = dict(shared)
        m["x"] = np.ascontiguousarray(x[c * per:(c + 1) * per].reshape(per * S, D))
        in_maps.append(m)
    res = run_bass_kernel_spmd(nc, in_maps, core_ids=list(range(NCORES)))
    out = np.concatenate([np.asarray(r["out"]).reshape(per, S, D) for r in res.results], axis=0)
    return out.astype(np.float32)
```

```python
import numpy as np
import concourse.bass as bass
import concourse.mybir as mybir
from concourse.bass_utils import run_bass_kernel_spmd

F32 = mybir.dt.float32
BF16 = mybir.dt.bfloat16
AF = mybir.ActivationFunctionType
ALU = mybir.AluOpType

D = 1024
S = 2048
DFF = 2816
NF = 22
DEPTH = 4
EPS = 1e-6
TB = 512
NB = S // TB
NCORES = 8
NEGB = -30000.0
SLOT = 1408
NSLOT = 5
PRE_LOOK = 80

def _layout():
    units = {}
    off = 0

    def add(name, n):
        nonlocal off
        units[name] = (off, n)
        off += n

    def add_ffn(i, w):
        for f in range(NF):
            add(("G", i, w, f), 1024)
            add(("U", i, w, f), 1024)
        for d in range(8):
            for hh in range(2):
                add(("D", i, w, d, hh), 1408)

    for i in range(DEPTH):
        add_ffn(i, 0)
        j = i // 2
        if i % 2 == 0:
            for nm in ["ak", "bk0", "bk1", "bk2", "bk3", "av", "bv0", "bv1", "bv2", "bv3",
                       "aq0", "aq1", "aq2", "aq3", "bq0", "bq1", "bq2", "bq3"]:
                add(("AB", j, nm), 1024)
            for d in range(8):
                add(("ABO", j, d), 1024)
        else:
            for u in range(20):
                add(("CD", j, u), 1024)
            for d in range(8):
                add(("CDO", j, d), 1024)
        add_ffn(i, 1)
    return units, off


UNITS, NCW = _layout()
CVT = 512
NCW_PAD = ((NCW + CVT - 1) // CVT) * CVT

PV = {}
_o = 0
for _i in range(DEPTH):
    for _w in range(2):
        PV[("ffn", _i, _w)] = _o; _o += 8
    PV[("mix", _i)] = _o; _o += 8
PV["final"] = _o; _o += 8
for _j in range(2):
    PV[("qn", _j)] = _o; _o += 1
    PV[("kn", _j)] = _o; _o += 1
    PV[("dg", _j)] = _o; _o += 4
    PV[("db", _j)] = _o; _o += 4
    PV[("c3", _j)] = _o; _o += 12
    PV[("c31", _j)] = _o; _o += 124
NPV = _o
CM_ID = 0; CM_ROT = 128; CM_MEAN = 256; CM_ONES = 384; CM_BD = 512
NCM = 640


def _na_kts(m):
    def rs(r):
        return min(max(r - 4, 0), 24)
    lo = rs(2 * m) // 2
    hi = (rs(2 * m + 1) + 7) // 2
    return list(range(lo, hi + 1))


NA_BIDX = {}
for _m in range(16):
    NA_BIDX[_m] = {0: 1, 1: 2, 14: 3, 15: 4}.get(_m, 0)


def _pack_weights(inp):
    wall = np.zeros((128, NCW_PAD), np.float32)

    def put(name, blk):
        o, n = UNITS[name]
        p = blk.shape[0]
        wall[:p, o:o + blk.shape[1]] = blk

    for i in range(DEPTH):
        for w in range(2):
            g = inp["ffn_w_gate"][i, w].reshape(8, 128, NF, 128)
            u = inp["ffn_w_up"][i, w].reshape(8, 128, NF, 128)
            g = np.ascontiguousarray(g.transpose(2, 1, 0, 3)).reshape(NF, 128, 1024)
            u = np.ascontiguousarray(u.transpose(2, 1, 0, 3)).reshape(NF, 128, 1024)
            dn = inp["ffn_w_down"][i, w].reshape(NF, 128, 8, 128)
            dn = np.ascontiguousarray(dn.transpose(2, 1, 0, 3))
            for f in range(NF):
                put(("G", i, w, f), g[f])
                put(("U", i, w, f), u[f])
            for d in range(8):
                for hh in range(2):
                    put(("D", i, w, d, hh), dn[d][:, hh * 11:(hh + 1) * 11, :].reshape(128, 1408))
        j = i // 2
        if i % 2 == 0:
            wi = inp["ab_w_in"][j]

            def unit_cols(cols):
                b = wi[:, cols].reshape(8, 128, 128)
                return np.ascontiguousarray(b.transpose(1, 0, 2)).reshape(128, 1024)

            put(("AB", j, "ak"), unit_cols(np.arange(512, 640)))
            put(("AB", j, "av"), unit_cols(np.arange(640, 768)))
            for c in range(4):
                put(("AB", j, "bk%d" % c), unit_cols(np.arange(1280 + 128 * c, 1280 + 128 * (c + 1))))
                put(("AB", j, "bv%d" % c), unit_cols(np.arange(1792 + 128 * c, 1792 + 128 * (c + 1))))
                put(("AB", j, "bq%d" % c), unit_cols(np.arange(768 + 128 * c, 768 + 128 * (c + 1))))
                cols = np.concatenate([np.arange(64 * c, 64 * c + 64), np.arange(256 + 64 * c, 256 + 64 * c + 64)])
                put(("AB", j, "aq%d" % c), unit_cols(cols))
            wo = inp["ab_w_out"][j]
            rows = []
            for pr in range(8):
                if pr < 4:
                    rows.append(np.concatenate([np.arange(pr * 64, pr * 64 + 64), np.arange((4 + pr) * 64, (4 + pr) * 64 + 64)]))
                else:
                    rows.append(np.arange(512 + (pr - 4) * 128, 512 + (pr - 4) * 128 + 128))
            rows = np.stack(rows)
            wsel = wo[rows].reshape(8, 128, 8, 128)
            wsel = np.ascontiguousarray(wsel.transpose(2, 1, 0, 3))
            for d in range(8):
                put(("ABO", j, d), wsel[d].reshape(128, 1024))
        else:
            wi = inp["cd_w_in"][j]
            for u in range(20):
                b = wi[:, 128 * u:128 * (u + 1)].reshape(8, 128, 128)
                put(("CD", j, u), np.ascontiguousarray(b.transpose(1, 0, 2)).reshape(128, 1024))
            wo = inp["cd_w_out"][j].reshape(8, 128, 8, 128)
            wo = np.ascontiguousarray(wo.transpose(2, 1, 0, 3))
            for d in range(8):
                put(("CDO", j, d), wo[d].reshape(128, 1024))
    return wall


def _pack_pvec(inp):
    pv = np.zeros((128, NPV), np.float32)
    for i in range(DEPTH):
        for w in range(2):
            pv[:, PV[("ffn", i, w)]:PV[("ffn", i, w)] + 8] = inp["ffn_norm"][i, w].reshape(8, 128).T
        pv[:, PV[("mix", i)]:PV[("mix", i)] + 8] = inp["mix_norm"][i].reshape(8, 128).T
    pv[:, PV["final"]:PV["final"] + 8] = inp["final_norm"].reshape(8, 128).T
    for j in range(2):
        pv[:, PV[("qn", j)]] = np.tile(inp["a_q_norm"][j], 2)
        pv[:, PV[("kn", j)]] = np.tile(inp["a_k_norm"][j], 2)
        pv[:, PV[("dg", j)]:PV[("dg", j)] + 4] = inp["d_norm_g"][j].reshape(4, 128).T
        pv[:, PV[("db", j)]:PV[("db", j)] + 4] = inp["d_norm_b"][j].reshape(4, 128).T
        c3 = inp["c_conv_w"][j].reshape(3, 4, 128)
        pv[:, PV[("c3", j)]:PV[("c3", j)] + 12] = c3.transpose(2, 1, 0).reshape(128, 12)
        c31 = inp["d_conv_w"][j].reshape(31, 4, 128)
        pv[:, PV[("c31", j)]:PV[("c31", j)] + 124] = c31.transpose(2, 1, 0).reshape(128, 124)
    return pv


def _const_mats():
    cm = np.zeros((128, NCM), np.float32)
    cm[:, CM_ID:CM_ID + 128] = np.eye(128, dtype=np.float32)
    rot = np.zeros((128, 128), np.float32)
    for i in range(64):
        rot[2 * i + 1, 2 * i] = -1.0
        rot[2 * i, 2 * i + 1] = 1.0
    cm[:, CM_ROT:CM_ROT + 128] = rot
    cm[:, CM_MEAN:CM_MEAN + 128] = 1.0 / 512.0
    cm[:, CM_ONES:CM_ONES + 128] = 1.0
    bd = np.zeros((128, 128), np.float32)
    bd[:64, :64] = 1.0
    bd[64:, 64:] = 1.0
    cm[:, CM_BD:CM_BD + 128] = bd
    return cm


def _rope_tables():
    t = np.arange(S)
    row = (t // 64).astype(np.float32)
    col = (t % 64).astype(np.float32)
    half = 32
    freqs = (np.float32(10000.0) ** (-np.arange(0, half, 2, dtype=np.float32) / np.float32(half))).astype(np.float32)
    ang = np.concatenate([row[:, None] * freqs, col[:, None] * freqs], axis=-1).astype(np.float32)
    cos = np.cos(ang).astype(np.float32)
    sin = np.sin(ang).astype(np.float32)
    idx = (np.arange(128) % 64) // 2
    tab = np.zeros((2, 128, S), np.float32)
    tab[0] = cos[:, idx].T
    tab[1] = sin[:, idx].T
    return tab


def _na_bias(rpb):
    out = np.full((2, 8, 5, 128, 5, 128), NEGB, np.float32)
    reps = {0: 4, 1: 0, 2: 1, 3: 14, 4: 15}
    for g, m in reps.items():
        kts = _na_kts(m)
        qt = np.arange(128 * m, 128 * m + 128)
        qr = qt // 64
        qc = qt % 64
        rs = np.clip(qr - 4, 0, 24)
        wcs = np.clip(qc - 8, 0, 48)
        for i, kt in enumerate(kts):
            k = np.arange(128 * kt, 128 * kt + 128)
            kr = (k // 64)[:, None]
            kc = (k % 64)[:, None]
            valid = (kr >= rs[None]) & (kr < rs[None] + 8) & (kc >= wcs[None]) & (kc < wcs[None] + 16)
            ro = np.clip(kr - qr[None] + 7, 0, 14)
            co = np.clip(kc - qc[None], -15, 15) + 15
            vals = rpb[:, :, ro, co]
            out[:, :, g, :, i, :] = np.where(valid[None, None], vals, np.float32(NEGB))
    return out


ENGS = ["pe", "act", "dve", "pool", "sp"]


class Tile:
    __slots__ = ("name", "w", "r", "rd", "const")

    def __init__(self, name, const=False):
        self.name = name
        self.w = None
        self.r = {}
        self.rd = []
        self.const = const


class DSem:
    __slots__ = ("h", "count", "idx")

    def __init__(self, idx):
        self.idx = idx
        self.h = None
        self.count = 0


class Ins:
    __slots__ = ("eng", "fn", "deps", "idx", "is_dma", "dsem", "val", "need_inc")


class Prog:
    def __init__(self):
        self.streams = {e: [] for e in ENGS}
        self.pending_barrier = {e: [] for e in ENGS}
        self.dsems = []
        self.open_dmas = {}

    def new_dsem(self):
        d = DSem(len(self.dsems))
        self.dsems.append(d)
        return d

    def _mk(self, eng, fn, reads, writes, is_dma, dsem, extra=(), arena=True):
        ins = Ins()
        ins.eng = eng
        ins.fn = fn
        ins.is_dma = is_dma
        ins.dsem = dsem
        ins.need_inc = False
        ins.val = 0
        deps = []
        for t in reads:
            if t.w is not None:
                deps.append(t.w)
        for t in writes:
            if t.w is not None:
                deps.append(t.w)
            deps.extend(t.r.values())
            deps.extend(t.rd)
        deps.extend(extra)
        if self.pending_barrier[eng] and arena:
            deps.extend(self.pending_barrier[eng])
            self.pending_barrier[eng] = []
        ins.deps = deps
        ins.idx = len(self.streams[eng])
        self.streams[eng].append(ins)
        if is_dma:
            dsem.count += 16
            ins.val = dsem.count
            if arena:
                self.open_dmas[dsem.idx] = ins
        for t in reads:
            if t.const:
                continue
            if is_dma:
                t.rd.append(ins)
            else:
                t.r[eng] = ins
        for t in writes:
            t.w = ins
            t.r = {}
            t.rd = []
        return ins

    def op(self, eng, fn, reads=(), writes=()):
        return self._mk(eng, fn, reads, writes, False, None)

    def dma(self, out, in_, dsem, reads=(), writes=(), eng="sp", extra=(), arena=True):
        return self._mk(eng, (lambda e, o=out, i=in_: e.dma_start(out=o, in_=i)), reads, writes, True, dsem, extra, arena)

    def barrier(self):
        lasts = [self.streams[e][-1] for e in ["pe", "act", "dve"] if self.streams[e]]
        lasts += list(self.open_dmas.values())
        for e in ["pe", "act", "dve", "sp"]:
            self.pending_barrier[e] = list(lasts)

    def emit(self, nc, block, sems, end_fn):
        plans = {}
        for e in ENGS:
            waited = {}
            plan = []
            for ins in self.streams[e]:
                ws = []
                for d in ins.deps:
                    if d.is_dma:
                        key = ("d", d.dsem.idx)
                        v = d.val
                    else:
                        if d.eng == e and e == "pe":
                            continue
                        if d.eng == e and d.idx >= ins.idx:
                            continue
                        key = ("e", d.eng)
                        v = d.idx
                    if waited.get(key, -1) >= v:
                        continue
                    waited[key] = v
                    ws.append(d)
                    if not d.is_dma:
                        d.need_inc = True
                plan.append(ws)
            plans[e] = plan
        for e in ENGS:
            c = 0
            for ins in self.streams[e]:
                if ins.is_dma:
                    continue
                if ins.need_inc:
                    c += 1
                    ins.val = c
        engmap = {"pe": block.tensor, "act": block.scalar, "dve": block.vector, "pool": block.gpsimd, "sp": block.sync}

        def make_body(e):
            def body(eng):
                for ins, ws in zip(self.streams[e], plans[e]):
                    best = {}
                    for d in ws:
                        if d.is_dma:
                            k = ("d", d.dsem.idx)
                            h = d.dsem.h
                        else:
                            k = ("e", d.eng)
                            h = sems[d.eng]
                        if k not in best or best[k][1] < d.val:
                            best[k] = (h, d.val)
                    for h, v in best.values():
                        eng.wait_ge(h, v)
                    r = ins.fn(eng)
                    if ins.is_dma:
                        r.then_inc(ins.dsem.h, 16)
                    elif ins.need_inc:
                        r.then_inc(sems[e], 1)
                if end_fn is not None:
                    end_fn(e, eng)
            return body

        for e in ENGS:
            engmap[e](make_body(e))


class Ring:
    def __init__(self, prog, ring_ap_fn, wbf):
        self.prog = prog
        self.ap_fn = ring_ap_fn
        self.wbf = wbf
        self.tiles = [Tile("ring%d" % i) for i in range(NSLOT)]
        self.dsems = [prog.new_dsem() for _ in range(NSLOT)]
        self.seq = None
        self.pump_to = None
        self.pumped_unit = -1
        self.cvt_out = {}
        self.rec = []
        self.pos = 0
        self.issued = 0

    def _issue(self, k):
        name, parts = self.seq[k]
        off, n = UNITS[name]
        s = k % NSLOT
        extra = ()
        if self.cvt_out:
            cmax = (off + n - 1) // CVT
            extra = tuple(self.cvt_out[c] for c in (cmax, cmax - 1) if c in self.cvt_out)
        self.prog.dma(self.ap_fn(s)[0:parts, 0:n], self.wbf[0:parts, off:off + n], self.dsems[s],
                      reads=(), writes=(self.tiles[s],), extra=extra, arena=False)

    def get(self, name, parts=128):
        if self.seq is None:
            self.rec.append((name, parts))
            k = len(self.rec) - 1
            return self.tiles[k % NSLOT], self.ap_fn(k % NSLOT)
        k = self.pos
        assert self.seq[k] == (name, parts), (self.seq[k], name)
        if self.pump_to is not None:
            ku = min(len(self.seq) - 1, k + PRE_LOOK)
            if ku > self.pumped_unit:
                self.pumped_unit = ku
                o, n = UNITS[self.seq[ku][0]]
                self.pump_to((o + n - 1) // CVT)
        while self.issued < min(len(self.seq), k + NSLOT - 1):
            self._issue(self.issued)
            self.issued += 1
        self.pos += 1
        return self.tiles[k % NSLOT], self.ap_fn(k % NSLOT)


def _emit_program(P, T, cfg, recorded_seq):
    nseq = cfg["nseq"]
    stages = cfg["stages"]
    xT = T["xT"]
    arena = T["arena"]
    pvec = T["pvec"]
    cmat = T["cmat"]
    cbf = T["cbf"]
    ps = T["ps"]
    AW = T["AW"]

    consts = Tile("consts", const=True)
    pst = [Tile("ps%d" % i) for i in range(8)]
    psn = [0]

    def psum():
        i = 2 + psn[0] % 6
        psn[0] += 1
        return pst[i], ps[i]

    pan = [0]

    def psum_acc():
        i = pan[0] % 2
        pan[0] += 1
        return pst[i], ps[i]

    xt_tiles = [[Tile("x%d_%d" % (c, b)) for b in range(NB)] for c in range(8)]

    ring = Ring(P, lambda s: T["ring"][:, s * SLOT:(s + 1) * SLOT], T["wbf"])
    ring.seq = recorded_seq

    class Carver:
        def __init__(self):
            self.off = 0

        def f32(self, n):
            a = arena[:, self.off:self.off + n]
            self.off += n
            assert self.off <= AW, (self.off, AW)
            return a

        def bf(self, n):
            w = (n + 1) // 2
            a = arena[:, self.off:self.off + w].bitcast(BF16)
            self.off += w
            assert self.off <= AW, (self.off, AW)
            return a

    ID = cmat[:, CM_ID:CM_ID + 128]
    ROT = cmat[:, CM_ROT:CM_ROT + 128]
    MEANM = cmat[:, CM_MEAN:CM_MEAN + 128]
    ONESB = cbf[:, 0:128]
    BDB = cbf[:, 128:256]
    IDB = cbf[:, 256:384]

    cds = P.new_dsem()
    P.dma(pvec, T["pvec_d"], cds, writes=(consts,))
    P.dma(cmat, T["cmat_d"], cds, writes=(consts,))
    P.op("dve", lambda e: e.tensor_copy(out=cbf[:, 0:128], in_=cmat[:, CM_ONES:CM_ONES + 128]), reads=(consts,), writes=(consts,))
    P.op("dve", lambda e: e.tensor_copy(out=cbf[:, 128:256], in_=cmat[:, CM_BD:CM_BD + 128]), reads=(consts,), writes=(consts,))
    P.op("dve", lambda e: e.tensor_copy(out=cbf[:, 256:384], in_=cmat[:, CM_ID:CM_ID + 128]), reads=(consts,), writes=(consts,))

    if cfg.get("prepass", True) and recorded_seq is not None:
        NBUF = 3
        stin = T["cvt_in"]
        stout = T["cvt_out"]
        st_in = [stin[:, b * CVT:(b + 1) * CVT] for b in range(NBUF)]
        st_out = [stout[:, b * CVT:(b + 1) * CVT] for b in range(NBUF)]
        t_in = [Tile("cvi%d" % i) for i in range(NBUF)]
        t_out = [Tile("cvo%d" % i) for i in range(NBUF)]
        ds_in = [P.new_dsem() for _ in range(NBUF)]
        ds_out = [P.new_dsem() for _ in range(NBUF)]
        klo, khi = cfg.get("prepass_range", (0, NCW_PAD // CVT))
        pst_ = {"next_in": klo, "next_out": klo}

        def emit_out(k):
            b = k % NBUF
            ring.cvt_out[k] = P.dma(T["wbf"][:, k * CVT:(k + 1) * CVT], st_out[b], ds_out[b], reads=(t_out[b],), arena=False)
            pst_["next_out"] = k + 1

        def pump_to(kneed):
            kneed = min(kneed, khi - 1)
            while pst_["next_out"] <= kneed:
                k = pst_["next_in"]
                if k < khi:
                    b = k % NBUF
                    P.dma(st_in[b], T["wall"][:, k * CVT:(k + 1) * CVT], ds_in[b], writes=(t_in[b],), arena=False)
                    P.op("pool", lambda e, b=b: e.tensor_copy(out=st_out[b], in_=st_in[b]), reads=(t_in[b],), writes=(t_out[b],))
                    pst_["next_in"] = k + 1
                if pst_["next_out"] <= pst_["next_in"] - 3 or pst_["next_in"] >= khi:
                    emit_out(pst_["next_out"])

        ring.pump_to = pump_to

    def rms_stats(cv_tmp, blk, out_rstd, rstd_tile):
        sq, sqt = cv_tmp
        pt, pa = psum()
        for c in range(8):
            i = c % 2
            P.op("act", lambda e, c=c, i=i: e.activation(out=sq[i], in_=xT[:, c, blk * TB:(blk + 1) * TB], func=AF.Square),
                 reads=(xt_tiles[c][blk],), writes=(sqt[i],))
            P.op("pe", lambda e, c=c, i=i: e.matmul(pa, ONESB, sq[i], start=(c == 0), stop=(c == 7)),
                 reads=(sqt[i], consts), writes=(pt,))
        P.op("act", lambda e: e.activation(out=out_rstd, in_=pa, func=AF.Ln, bias=EPS, scale=1.0 / D),
             reads=(pt,), writes=(rstd_tile,))
        P.op("act", lambda e: e.activation(out=out_rstd, in_=out_rstd, func=AF.Exp, scale=-0.5),
             reads=(rstd_tile,), writes=(rstd_tile,))

    def apply_norm(blk, gcol, rstd, rstd_tile, out_fn, out_tiles):
        for c in range(8):
            P.op("dve", lambda e, c=c: e.scalar_tensor_tensor(out=out_fn(c), in0=xT[:, c, blk * TB:(blk + 1) * TB],
                                                              scalar=pvec[:, gcol + c:gcol + c + 1], in1=rstd,
                                                              op0=ALU.mult, op1=ALU.mult),
                 reads=(xt_tiles[c][blk], rstd_tile, consts), writes=(out_tiles[c],))

    def load_x(s):
        cv = Carver()
        stg = [cv.f32(1024) for _ in range(2)]
        stt = [Tile("ldst%d" % i) for i in range(2)]
        sds = [P.new_dsem() for _ in range(2)]
        for tt in range(16):
            b = tt % 2
            P.dma(stg[b], T["x_d"][s * S + tt * 128:s * S + (tt + 1) * 128, :], sds[b], writes=(stt[b],))
            for half in range(2):
                pt, pa = psum()
                for q in range(4):
                    c = half * 4 + q
                    P.op("pe", lambda e, c=c, q=q, b=b, pa=pa: e.transpose(pa[:, q * 128:(q + 1) * 128], stg[b][:, c * 128:(c + 1) * 128], ID),
                         reads=(stt[b], consts), writes=(pt,))
                eg = "act" if half == 0 else "dve"
                dst = xT[:, half * 4:half * 4 + 4, tt * 128:(tt + 1) * 128]
                src = pa.rearrange("p (q t) -> p q t", q=4)
                wr = tuple(xt_tiles[half * 4 + q][tt // 4] for q in range(4))
                if eg == "act":
                    P.op("act", lambda e, dst=dst, src=src: e.copy(out=dst, in_=src), reads=(pt,), writes=wr)
                else:
                    P.op("dve", lambda e, dst=dst, src=src: e.tensor_copy(out=dst, in_=src), reads=(pt,), writes=wr)
        P.barrier()

    def store_out(s):
        cv = Carver()
        sq = [cv.bf(TB) for _ in range(2)]
        sqt = [Tile("fsq%d" % i) for i in range(2)]
        rstd = cv.f32(TB)
        rstd_t = Tile("frstd")
        xn = cv.f32(8 * TB).rearrange("p (c t) -> p c t", c=8)
        xn_t = [Tile("fxn%d" % c) for c in range(8)]
        stg = [cv.f32(1024) for _ in range(2)]
        stt = [Tile("fst%d" % i) for i in range(2)]
        ods = [P.new_dsem() for _ in range(2)]
        k = 0
        for blk in range(NB):
            rms_stats((sq, sqt), blk, rstd, rstd_t)
            apply_norm(blk, PV["final"], rstd, rstd_t, lambda c: xn[:, c, :], xn_t)
            for t4 in range(4):
                b = k % 2
                k += 1
                for half in range(2):
                    pt, pa = psum()
                    for q in range(4):
                        c = half * 4 + q
                        P.op("pe", lambda e, c=c, q=q, pa=pa, t4=t4: e.transpose(pa[:, q * 128:(q + 1) * 128], xn[:, c, t4 * 128:(t4 + 1) * 128], ID),
                             reads=(xn_t[c], consts), writes=(pt,))
                    dst = stg[b][:, half * 512:(half + 1) * 512]
                    if half == 0:
                        P.op("act", lambda e, dst=dst, pa=pa: e.copy(out=dst, in_=pa), reads=(pt,), writes=(stt[b],))
                    else:
                        P.op("dve", lambda e, dst=dst, pa=pa: e.tensor_copy(out=dst, in_=pa), reads=(pt,), writes=(stt[b],))
                tt = blk * 4 + t4
                P.dma(T["out_d"][s * S + tt * 128:s * S + (tt + 1) * 128, :], stg[b], ods[b], reads=(stt[b],))
        P.barrier()

    def ffn(i, w):
        cv = Carver()
        xn = cv.bf(8 * 1024).rearrange("p (c t) -> p c t", c=8)
        h = cv.bf(NF * 1024).rearrange("p (f t) -> p f t", f=NF)
        sq = [cv.bf(TB) for _ in range(2)]
        sqt = [Tile("sq%d" % k) for k in range(2)]
        rstd = [cv.f32(TB) for _ in range(2)]
        rstd_t = [Tile("rstd%d" % k) for k in range(2)]
        sg = [cv.f32(TB) for _ in range(3)]
        sg_t = [Tile("sg%d" % k) for k in range(3)]
        xn_t = [[Tile("xn%d_%d" % (c, b)) for b in range(2)] for c in range(8)]
        h_t = [[Tile("h%d_%d" % (f, b)) for b in range(2)] for f in range(NF)]
        gcol = PV[("ffn", i, w)]
        nsg = 0
        sq8 = [[cv.bf(TB) for _ in range(8)] for _ in range(2)]
        sq8t = [[Tile("sq8_%d_%d" % (b, c)) for c in range(8)] for b in range(2)]

        def norm_sq(half):
            for b2 in range(2):
                blk = half * 2 + b2
                for c in range(8):
                    P.op("act", lambda e, c=c, b2=b2, blk=blk: e.activation(out=sq8[b2][c], in_=xT[:, c, blk * TB:(blk + 1) * TB], func=AF.Square),
                         reads=(xt_tiles[c][blk],), writes=(sq8t[b2][c],))

        def norm_mm(half):
            for b2 in range(2):
                blk = half * 2 + b2
                pt, pa = psum()
                for c in range(8):
                    P.op("pe", lambda e, c=c, b2=b2, pa=pa: e.matmul(pa, ONESB, sq8[b2][c], start=(c == 0), stop=(c == 7)),
                         reads=(sq8t[b2][c], consts), writes=(pt,))
                P.op("act", lambda e, b2=b2, pa=pa: e.activation(out=rstd[b2], in_=pa, func=AF.Ln, bias=EPS, scale=1.0 / D),
                     reads=(pt,), writes=(rstd_t[b2],))
                P.op("act", lambda e, b2=b2: e.activation(out=rstd[b2], in_=rstd[b2], func=AF.Exp, scale=-0.5),
                     reads=(rstd_t[b2],), writes=(rstd_t[b2],))
                apply_norm(blk, gcol, rstd[b2], rstd_t[b2], lambda c, b2=b2: xn[:, c, b2 * TB:(b2 + 1) * TB],
                           [xn_t[c][b2] for c in range(8)])

        norm_sq(0)
        norm_mm(0)
        for half in range(2):
            for f in range(NF):
                gt, ga = ring.get(("G", i, w, f))
                ut, ua = ring.get(("U", i, w, f))
                for b2 in range(2):
                    pgt, pga = psum()
                    put, pua = psum()
                    for k in range(8):
                        P.op("pe", lambda e, k=k, b2=b2, pga=pga, ga=ga: e.matmul(pga, ga[:, k * 128:(k + 1) * 128], xn[:, k, b2 * TB:(b2 + 1) * TB], start=(k == 0), stop=(k == 7)),
                             reads=(gt, xn_t[k][b2]), writes=(pgt,))
                    for k in range(8):
                        P.op("pe", lambda e, k=k, b2=b2, pua=pua, ua=ua: e.matmul(pua, ua[:, k * 128:(k + 1) * 128], xn[:, k, b2 * TB:(b2 + 1) * TB], start=(k == 0), stop=(k == 7)),
                             reads=(ut, xn_t[k][b2]), writes=(put,))
                    si = nsg % 3
                    nsg += 1
                    P.op("act", lambda e, si=si, pga=pga: e.activation(out=sg[si], in_=pga, func=AF.Silu), reads=(pgt,), writes=(sg_t[si],))
                    P.op("dve", lambda e, si=si, pua=pua, f=f, b2=b2: e.tensor_tensor(out=h[:, f, b2 * TB:(b2 + 1) * TB], in0=pua, in1=sg[si], op=ALU.mult),
                         reads=(put, sg_t[si]), writes=(h_t[f][b2],))
            if half == 0:
                norm_sq(1)
            for d in range(8):
                if half == 0 and d == 1:
                    norm_mm(1)
                d0t, d0a = ring.get(("D", i, w, d, 0))
                d1t, d1a = ring.get(("D", i, w, d, 1))
                for b2 in range(2):
                    blk = half * 2 + b2
                    pyt, pya = psum()
                    for f in range(NF):
                        dt_, da = (d0t, d0a) if f < 11 else (d1t, d1a)
                        fi = f % 11
                        P.op("pe", lambda e, f=f, fi=fi, da=da, b2=b2, pya=pya: e.matmul(pya, da[:, fi * 128:(fi + 1) * 128], h[:, f, b2 * TB:(b2 + 1) * TB], start=(f == 0), stop=(f == NF - 1)),
                             reads=(dt_, h_t[f][b2]), writes=(pyt,))
                    xs = xT[:, d, blk * TB:(blk + 1) * TB]
                    P.op("dve", lambda e, xs=xs, pya=pya: e.scalar_tensor_tensor(out=xs, in0=pya, scalar=0.5, in1=xs, op0=ALU.mult, op1=ALU.add),
                         reads=(pyt, xt_tiles[d][blk]), writes=(xt_tiles[d][blk],))
        P.barrier()

    def mixer_cd(i):
        j = i // 2
        cv = Carver()
        PW = 2052
        GW = 2080
        pT = cv.bf(4 * PW).rearrange("p (c t) -> p c t", c=4)
        gT = cv.bf(4 * GW).rearrange("p (c t) -> p c t", c=4)
        cbT = cv.bf(4 * S).rearrange("p (c t) -> p c t", c=4)
        p_t = [[Tile("p%d_%d" % (c, b)) for b in range(NB)] for c in range(4)]
        g_t = [[Tile("g%d_%d" % (c, b)) for b in range(NB)] for c in range(4)]
        cb_t = [[Tile("cb%d_%d" % (c, b)) for b in range(NB)] for c in range(4)]
        pad_t = Tile("pads")
        dg3 = cv.bf(12 * 128).rearrange("p (n m) -> p n m", n=12)
        dg31 = cv.bf(124 * 128).rearrange("p (n m) -> p n m", n=124)
        dg_t = Tile("diag")
        mark = cv.off
        hxb = [cv.bf(8 * TB).rearrange("p (c t) -> p c t", c=8) for _ in range(2)]
        hx_t = [[Tile("hx%d_%d" % (b, c)) for c in range(8)] for b in range(2)]
        sq = [cv.bf(TB) for _ in range(2)]
        sqt = [Tile("sq%d" % k) for k in range(2)]
        rstd = cv.f32(TB)
        rstd_t = Tile("rstd")
        tm = [cv.f32(TB) for _ in range(4)]
        tm_t = [Tile("tm%d" % k) for k in range(4)]
        for c in range(4):
            P.op("dve", lambda e, c=c: e.memset(pT[:, c, 0:2], 0.0), writes=(pad_t,))
            P.op("dve", lambda e, c=c: e.memset(pT[:, c, 2 + S:PW], 0.0), writes=(pad_t,))
            P.op("dve", lambda e, c=c: e.memset(gT[:, c, 0:16], 0.0), writes=(pad_t,))
            P.op("dve", lambda e, c=c: e.memset(gT[:, c, 16 + S:GW], 0.0), writes=(pad_t,))
        for c in range(4):
            for k in range(3):
                col = PV[("c3", j)] + c * 3 + k
                P.op("dve", lambda e, c=c, k=k, col=col: e.tensor_scalar(out=dg3[:, c * 3 + k, :], in0=IDB, scalar1=pvec[:, col:col + 1], scalar2=None, op0=ALU.mult),
                     reads=(consts,), writes=(dg_t,))
            for k in range(31):
                col = PV[("c31", j)] + c * 31 + k
                eg = "dve"
                P.op(eg, lambda e, c=c, k=k, col=col: e.tensor_scalar(out=dg31[:, c * 31 + k, :], in0=IDB, scalar1=pvec[:, col:col + 1], scalar2=None, op0=ALU.mult),
                     reads=(consts,), writes=(dg_t,))
        ntm = 0
        for blk in range(NB):
            hb = blk % 2
            rms_stats((sq, sqt), blk, rstd, rstd_t)
            apply_norm(blk, PV[("mix", i)], rstd, rstd_t, lambda c, hb=hb: hxb[hb][:, c, :], hx_t[hb])

            def proj(u, hb=hb):
                wt, wa = ring.get(("CD", j, u))
                pt, pa = psum()
                for k in range(8):
                    P.op("pe", lambda e, k=k, wa=wa, pa=pa, hb=hb: e.matmul(pa, wa[:, k * 128:(k + 1) * 128], hxb[hb][:, k, :], start=(k == 0), stop=(k == 7)),
                         reads=(wt, hx_t[hb][k]), writes=(pt,))
                return pt, pa

            for c in range(4):
                pt1, pa1 = proj(0 + c)
                ti = ntm % 4; ntm += 1
                P.op("act", lambda e, ti=ti, pa1=pa1: e.copy(out=tm[ti], in_=pa1), reads=(pt1,), writes=(tm_t[ti],))
                pt2, pa2 = proj(8 + c)
                P.op("dve", lambda e, ti=ti, pa2=pa2, c=c, blk=blk: e.tensor_tensor(out=pT[:, c, 2 + blk * TB:2 + (blk + 1) * TB], in0=pa2, in1=tm[ti], op=ALU.mult),
                     reads=(pt2, tm_t[ti]), writes=(p_t[c][blk],))
                pt3, pa3 = proj(4 + c)
                P.op("act", lambda e, pa3=pa3, c=c, blk=blk: e.copy(out=cbT[:, c, blk * TB:(blk + 1) * TB], in_=pa3), reads=(pt3,), writes=(cb_t[c][blk],))
                pt4, pa4 = proj(16 + c)
                ti = ntm % 4; ntm += 1
                P.op("act", lambda e, ti=ti, pa4=pa4: e.activation(out=tm[ti], in_=pa4, func=AF.Sigmoid), reads=(pt4,), writes=(tm_t[ti],))
                pt5, pa5 = proj(12 + c)
                P.op("dve", lambda e, ti=ti, pa5=pa5, c=c, blk=blk: e.tensor_tensor(out=gT[:, c, 16 + blk * TB:16 + (blk + 1) * TB], in0=pa5, in1=tm[ti], op=ALU.mult),
                     reads=(pt5, tm_t[ti]), writes=(g_t[c][blk],))
        P.barrier()
        cv.off = mark
        yb = cv.bf(8 * TB).rearrange("p (c t) -> p c t", c=8)
        yb_t = [Tile("yb%d" % c) for c in range(8)]
        cvv = cv.f32(4 * TB).rearrange("p (c t) -> p c t", c=4)
        cv_t = [Tile("cv%d" % c) for c in range(4)]
        sqf = [cv.f32(TB) for _ in range(2)]
        sqf_t = [Tile("sqf%d" % k) for k in range(2)]
        mean = cv.f32(TB); mean_t = Tile("mean")
        m2 = cv.f32(TB); m2_t = Tile("m2")
        rs2 = cv.f32(TB); rs2_t = Tile("rs2")
        tmB = [cv.f32(TB) for _ in range(3)]
        tmB_t = [Tile("tmb%d" % k) for k in range(3)]
        ntm = 0
        for blk in range(NB):
            nb_r = [b for b in (blk - 1, blk, blk + 1) if 0 <= b < NB]
            for c in range(4):
                pt, pa = psum()
                for k in range(3):
                    P.op("pe", lambda e, c=c, k=k, pa=pa, blk=blk: e.matmul(pa, dg3[:, c * 3 + k, :], pT[:, c, blk * TB + k + 1:blk * TB + k + 1 + TB], start=(k == 0), stop=(k == 2)),
                         reads=tuple(p_t[c][b] for b in nb_r) + (dg_t, pad_t), writes=(pt,))
                P.op("dve", lambda e, c=c, pa=pa, blk=blk: e.tensor_tensor(out=yb[:, c, :], in0=pa, in1=cbT[:, c, blk * TB:(blk + 1) * TB], op=ALU.mult),
                     reads=(pt, cb_t[c][blk]), writes=(yb_t[c],))
            for c in range(4):
                pt, pa = psum()
                for k in range(31):
                    P.op("pe", lambda e, c=c, k=k, pa=pa, blk=blk: e.matmul(pa, dg31[:, c * 31 + k, :], gT[:, c, blk * TB + k + 1:blk * TB + k + 1 + TB], start=(k == 0), stop=(k == 30)),
                         reads=tuple(g_t[c][b] for b in nb_r) + (dg_t, pad_t), writes=(pt,))
                P.op("act", lambda e, c=c, pa=pa: e.copy(out=cvv[:, c, :], in_=pa), reads=(pt,), writes=(cv_t[c],))
            pmt, pma = psum()
            for c in range(4):
                P.op("pe", lambda e, c=c, pma=pma: e.matmul(pma, MEANM, cvv[:, c, :], start=(c == 0), stop=(c == 3)),
                     reads=(cv_t[c], consts), writes=(pmt,))
            pqt, pqa = psum()
            for c in range(4):
                si = c % 2
                P.op("act", lambda e, c=c, si=si: e.activation(out=sqf[si], in_=cvv[:, c, :], func=AF.Square), reads=(cv_t[c],), writes=(sqf_t[si],))
                P.op("pe", lambda e, c=c, si=si, pqa=pqa: e.matmul(pqa, MEANM, sqf[si], start=(c == 0), stop=(c == 3)),
                     reads=(sqf_t[si], consts), writes=(pqt,))
            P.op("act", lambda e, pma=pma: e.copy(out=mean, in_=pma), reads=(pmt,), writes=(mean_t,))
            P.op("dve", lambda e: e.tensor_tensor(out=m2, in0=mean, in1=mean, op=ALU.mult), reads=(mean_t,), writes=(m2_t,))
            P.op("dve", lambda e, pqa=pqa: e.tensor_tensor(out=rs2, in0=pqa, in1=m2, op=ALU.subtract), reads=(pqt, m2_t), writes=(rs2_t,))
            P.op("act", lambda e: e.activation(out=rs2, in_=rs2, func=AF.Ln, bias=EPS, scale=1.0), reads=(rs2_t,), writes=(rs2_t,))
            P.op("act", lambda e: e.activation(out=rs2, in_=rs2, func=AF.Exp, scale=-0.5), reads=(rs2_t,), writes=(rs2_t,))
            for c in range(4):
                ti = ntm % 3; ntm += 1
                P.op("dve", lambda e, c=c, ti=ti: e.tensor_tensor(out=tmB[ti], in0=cvv[:, c, :], in1=mean, op=ALU.subtract),
                     reads=(cv_t[c], mean_t), writes=(tmB_t[ti],))
                P.op("dve", lambda e, ti=ti: e.tensor_tensor(out=tmB[ti], in0=tmB[ti], in1=rs2, op=ALU.mult),
                     reads=(tmB_t[ti], rs2_t), writes=(tmB_t[ti],))
                gc = PV[("dg", j)] + c
                bc = PV[("db", j)] + c
                P.op("act", lambda e, c=c, ti=ti, gc=gc, bc=bc: e.activation(out=yb[:, 4 + c, :], in_=tmB[ti], func=AF.Silu, bias=pvec[:, bc:bc + 1], scale=pvec[:, gc:gc + 1]),
                     reads=(tmB_t[ti], consts), writes=(yb_t[4 + c],))
            for d in range(8):
                wt, wa = ring.get(("CDO", j, d))
                pt, pa = psum()
                for cc in range(8):
                    P.op("pe", lambda e, cc=cc, wa=wa, pa=pa: e.matmul(pa, wa[:, cc * 128:(cc + 1) * 128], yb[:, cc, :], start=(cc == 0), stop=(cc == 7)),
                         reads=(wt, yb_t[cc]), writes=(pt,))
                xs = xT[:, d, blk * TB:(blk + 1) * TB]
                P.op("dve", lambda e, xs=xs, pa=pa: e.tensor_tensor(out=xs, in0=pa, in1=xs, op=ALU.add),
                     reads=(pt, xt_tiles[d][blk]), writes=(xt_tiles[d][blk],))
        P.barrier()

    def mixer_ab(i):
        j = i // 2
        scale = 0.125
        cv = Carver()
        akT = cv.bf(S)
        bkT = cv.bf(4 * S).rearrange("p (c t) -> p c t", c=4)
        vaug2d = cv.bf(16 * 5 * 192)
        vaug = vaug2d.rearrange("p (t r e) -> p t r e", t=16, r=5)
        vaug5 = vaug2d.rearrange("p (t r k e) -> p t r k e", t=16, r=5, k=3)
        ak_t = [Tile("ak%d" % b) for b in range(NB)]
        bk_t = [[Tile("bk%d_%d" % (c, b)) for b in range(NB)] for c in range(4)]
        v_t = [Tile("v%d" % t) for t in range(16)]
        hxb = [cv.bf(8 * TB).rearrange("p (c t) -> p c t", c=8) for _ in range(1)]
        hx_t = [[Tile("hx%d_%d" % (b, c)) for c in range(8)] for b in range(2)]
        sq = [cv.bf(TB) for _ in range(2)]
        sqt = [Tile("sq%d" % k) for k in range(2)]
        rstd = cv.f32(TB); rstd_t = Tile("rstd")
        tabs = cv.f32(2 * TB).rearrange("p (a t) -> p a t", a=2)
        tab_t = Tile("tab")
        tab_ds = P.new_dsem()
        qsb = [cv.f32(TB) for _ in range(2)]; qsb_t = [Tile("qsb%d" % k) for k in range(2)]
        sqh = [cv.bf(TB) for _ in range(2)]; sqh_t = [Tile("sqh%d" % k) for k in range(2)]
        rsh = [cv.f32(TB) for _ in range(2)]; rsh_t = [Tile("rsh%d" % k) for k in range(2)]
        qn = [cv.f32(TB) for _ in range(2)]; qn_t = [Tile("qn%d" % k) for k in range(2)]
        markA = cv.off
        hxb.append(cv.bf(8 * TB).rearrange("p (c t) -> p c t", c=8))

        P.op("dve", lambda e: e.memset(vaug2d, 1.0), writes=tuple(v_t))

        def load_tabs(blk):
            P.dma(tabs, T["rope_d"][:, :, blk * TB:(blk + 1) * TB].rearrange("a p t -> p a t"), tab_ds, writes=(tab_t,))

        def rope_s1(pt, pa, k):
            P.op("act", lambda e: e.copy(out=qsb[k], in_=pa), reads=(pt,), writes=(qsb_t[k],))
            P.op("act", lambda e: e.activation(out=sqh[k], in_=pa, func=AF.Square), reads=(pt,), writes=(sqh_t[k],))

        def rope_s2(k, gcol):
            st, sa = psum()
            P.op("pe", lambda e: e.matmul(sa, BDB, sqh[k], start=True, stop=True), reads=(sqh_t[k], consts), writes=(st,))
            P.op("act", lambda e: e.activation(out=rsh[k], in_=sa, func=AF.Ln, bias=EPS, scale=1.0 / 64), reads=(st,), writes=(rsh_t[k],))
            P.op("act", lambda e: e.activation(out=rsh[k], in_=rsh[k], func=AF.Exp, scale=-0.5), reads=(rsh_t[k],), writes=(rsh_t[k],))
            P.op("dve", lambda e: e.scalar_tensor_tensor(out=qn[k], in0=qsb[k], scalar=pvec[:, gcol:gcol + 1], in1=rsh[k], op0=ALU.mult, op1=ALU.mult),
                 reads=(qsb_t[k], rsh_t[k], consts), writes=(qn_t[k],))

        def rope_s3(k, out_ap, out_tile):
            rt, ra = psum()
            P.op("pe", lambda e: e.matmul(ra, ROT, qn[k], start=True, stop=True), reads=(qn_t[k], consts), writes=(rt,))
            P.op("dve", lambda e: e.tensor_tensor(out=rsh[k], in0=qn[k], in1=tabs[:, 0, :], op=ALU.mult), reads=(qn_t[k], tab_t), writes=(rsh_t[k],))
            P.op("dve", lambda e: e.tensor_tensor(out=qsb[k], in0=ra, in1=tabs[:, 1, :], op=ALU.mult), reads=(rt, tab_t), writes=(qsb_t[k],))
            P.op("dve", lambda e: e.tensor_tensor(out=out_ap, in0=rsh[k], in1=qsb[k], op=ALU.add), reads=(rsh_t[k], qsb_t[k]), writes=(out_tile,))

        def proj_fm(name, hb):
            wt, wa = ring.get(("AB", j, name))
            pt, pa = psum()
            for k in range(8):
                P.op("pe", lambda e, k=k, wa=wa, pa=pa: e.matmul(pa, wa[:, k * 128:(k + 1) * 128], hxb[hb][:, k, :], start=(k == 0), stop=(k == 7)),
                     reads=(wt, hx_t[hb][k]), writes=(pt,))
            return pt, pa

        for blk in range(NB):
            hb = blk % 2
            rms_stats((sq, sqt), blk, rstd, rstd_t)
            apply_norm(blk, PV[("mix", i)], rstd, rstd_t, lambda c, hb=hb: hxb[hb][:, c, :], hx_t[hb])
            load_tabs(blk)
            pt, pa = proj_fm("ak", hb)
            rope_s1(pt, pa, 0)
            for c in range(4):
                pt, pa = proj_fm("bk%d" % c, hb)
                if c % 2 == 0:
                    P.op("act", lambda e, pa=pa, c=c, blk=blk: e.copy(out=bkT[:, c, blk * TB:(blk + 1) * TB], in_=pa), reads=(pt,), writes=(bk_t[c][blk],))
                else:
                    P.op("dve", lambda e, pa=pa, c=c, blk=blk: e.tensor_copy(out=bkT[:, c, blk * TB:(blk + 1) * TB], in_=pa), reads=(pt,), writes=(bk_t[c][blk],))
                if c == 0:
                    rope_s2(0, PV[("kn", j)])
                if c == 2:
                    rope_s3(0, akT[:, blk * TB:(blk + 1) * TB], ak_t[blk])
            for ui, name in enumerate(["av", "bv0", "bv1", "bv2", "bv3"]):
                wt, wa = ring.get(("AB", j, name))
                pt, pa = psum()
                for t4 in range(4):
                    for k in range(8):
                        P.op("pe", lambda e, k=k, wa=wa, pa=pa, t4=t4, hb=hb: e.matmul(pa[:, t4 * 128:(t4 + 1) * 128], hxb[hb][:, k, t4 * 128:(t4 + 1) * 128], wa[:, k * 128:(k + 1) * 128], start=(k == 0), stop=(k == 7)),
                             reads=(wt, hx_t[hb][k]), writes=(pt,))
                src = pa.rearrange("p (t h e) -> p t h e", t=4, h=2)
                dst = vaug5[:, blk * 4:(blk + 1) * 4, ui, 0:3:2, :]
                wr = tuple(v_t[blk * 4 + t4] for t4 in range(4))
                if ui % 2 == 0:
                    P.op("dve", lambda e, src=src, dst=dst: e.tensor_copy(out=dst, in_=src), reads=(pt,), writes=wr)
                else:
                    P.op("act", lambda e, src=src, dst=dst: e.copy(out=dst, in_=src), reads=(pt,), writes=wr)
        P.barrier()
        cv.off = markA
        qb = [cv.bf(8 * TB).rearrange("p (c t) -> p c t", c=8) for _ in range(1)]
        qb_t = [[Tile("qb%d_%d" % (b, c)) for c in range(8)] for b in range(1)]
        yb = cv.bf(8 * TB).rearrange("p (h t) -> p h t", h=8)
        yb_t = [Tile("yb%d" % h) for h in range(16)]
        NPB = 4
        pbuf = [cv.bf(640) for _ in range(NPB)]
        pb_t = [Tile("pb%d" % k) for k in range(NPB)]
        biasb = [cv.f32(640).rearrange("p (k q) -> p k q", k=5) for _ in range(2)]
        bias_t = [Tile("bias%d" % k) for k in range(2)]
        bias_ds = [P.new_dsem() for _ in range(2)]
        NNAT = 3
        nat = [cv.f32(640) for _ in range(NNAT)]
        nat_t = [Tile("nat%d" % k) for k in range(NNAT)]
        rt_ = cv.f32(TB); rt_t = Tile("rt")
        npb = [0]
        nbias = [0]
        nnat = [0]

        def norm_b(blk):
            rms_stats((sq, sqt), blk, rstd, rstd_t)
            apply_norm(blk, PV[("mix", i)], rstd, rstd_t, lambda c: hxb[0][:, c, :], hx_t[0])
            load_tabs(blk)

        norm_b(0)
        for blk in range(NB):
            hb = 0
            def bq_proj(c):
                pt, pa = proj_fm("bq%d" % c, 0)
                P.op("dve", lambda e, pa=pa, c=c: e.tensor_copy(out=qb[0][:, 4 + c, :], in_=pa), reads=(pt,), writes=(qb_t[0][4 + c],))

            def q_pair(c0):
                for k in range(2):
                    pt, pa = proj_fm("aq%d" % (c0 + k), 0)
                    rope_s1(pt, pa, k)
                for k in range(2):
                    rope_s2(k, PV[("qn", j)])
                bq_proj(c0)
                bq_proj(c0 + 1)
                for k in range(2):
                    rope_s3(k, qb[0][:, c0 + k, :], qb_t[0][c0 + k])

            q_pair(0)

            def finish(ot, oa, pr, bside):
                o0, d0 = (64, 0) if bside else (0, 64)
                if pr >= 4:
                    P.op("act", lambda e: e.activation(out=rt_[d0:d0 + 64, :], in_=oa[d0:d0 + 64, :], func=AF.Ln), reads=(ot,), writes=(rt_t,))
                    P.op("act", lambda e: e.activation(out=rt_[d0:d0 + 64, :], in_=rt_[d0:d0 + 64, :], func=AF.Exp, scale=-1.0), reads=(rt_t,), writes=(rt_t,))
                else:
                    P.op("dve", lambda e: e.reciprocal(out=rt_[d0:d0 + 64, :], in_=oa[d0:d0 + 64, :]), reads=(ot,), writes=(rt_t,))
                P.op("dve", lambda e: e.tensor_tensor(out=yb[o0:o0 + 64, pr, :], in0=oa[o0:o0 + 64, :], in1=rt_[d0:d0 + 64, :], op=ALU.mult),
                     reads=(ot, rt_t), writes=(yb_t[pr * 2 + bside],))

            acc = {}

            def g_qk(hq, kt):
                c = hq % 4
                pb = 0 if hq < 4 else 64
                if kt == 0:
                    acc[("g", hq)] = psum_acc()
                st, sa = psum()
                P.op("pe", lambda e, kt=kt, pb=pb, c=c, sa=sa: e.matmul(sa, akT[pb:pb + 64, kt * 128:(kt + 1) * 128], qb[0][pb:pb + 64, c, :], start=True, stop=True),
                     reads=(ak_t[kt // 4], qb_t[0][c]), writes=(st,))
                pi = npb[0] % NPB; npb[0] += 1
                P.op("act", lambda e, pi=pi, sa=sa: e.activation(out=pbuf[pi][:, 0:TB], in_=sa, func=AF.Exp, scale=scale), reads=(st,), writes=(pb_t[pi],))
                return pi

            def g_pv(hq, kt, pi):
                ot, oa = acc[("g", hq)]
                vh = hq // 4
                P.op("pe", lambda e, kt=kt, vh=vh, pi=pi, oa=oa: e.matmul(oa, vaug[:, kt, 0, 64 * vh:64 * vh + 128], pbuf[pi][:, 0:TB], start=(kt == 0), stop=(kt == 15)),
                     reads=(v_t[kt], pb_t[pi]), writes=(ot,))
                if kt == 15:
                    finish(ot, oa, hq % 4, hq // 4)

            def n_qk(h, q4):
                c = h // 2
                pb = 64 * (h % 2)
                if q4 == 0:
                    acc[("n", h)] = psum_acc()
                m = blk * 4 + q4
                kts = _na_kts(m)
                g = NA_BIDX[m]
                need_load = (q4 == 0) or (g != NA_BIDX[m - 1])
                if need_load:
                    bi = nbias[0] % 2; nbias[0] += 1
                    P.dma(biasb[bi], T["bias_d"][j, h, g], bias_ds[bi], writes=(bias_t[bi],))
                bi = (nbias[0] - 1) % 2
                nk = len(kts)
                st, sa = psum()
                st2, sa2 = (None, None)
                if nk == 5:
                    st2, sa2 = psum()
                for ki, kt in enumerate(kts):
                    tgt = sa[:, ki * 128:(ki + 1) * 128] if ki < 4 else sa2[:, 0:128]
                    tt_ = st if ki < 4 else st2
                    P.op("pe", lambda e, kt=kt, pb=pb, c=c, tgt=tgt, q4=q4: e.matmul(tgt, bkT[pb:pb + 64, c, kt * 128:(kt + 1) * 128], qb[0][pb:pb + 64, 4 + c, q4 * 128:(q4 + 1) * 128], start=True, stop=True),
                         reads=(bk_t[c][kt // 4], qb_t[0][4 + c]), writes=(tt_,))
                ni = nnat[0] % NNAT; nnat[0] += 1
                n4 = min(nk, 4) * 128
                P.op("dve", lambda e, ni=ni, sa=sa, bi=bi, n4=n4: e.scalar_tensor_tensor(out=nat[ni][:, 0:n4], in0=sa[:, 0:n4], scalar=scale, in1=biasb[bi].rearrange("p k q -> p (k q)")[:, 0:n4], op0=ALU.mult, op1=ALU.add),
                     reads=(st, bias_t[bi]), writes=(nat_t[ni],))
                if nk == 5:
                    P.op("dve", lambda e, ni=ni, sa2=sa2, bi=bi: e.scalar_tensor_tensor(out=nat[ni][:, 512:640], in0=sa2[:, 0:128], scalar=scale, in1=biasb[bi][:, 4, :], op0=ALU.mult, op1=ALU.add),
                         reads=(st2, bias_t[bi]), writes=(nat_t[ni],))
                pi = npb[0] % NPB; npb[0] += 1
                P.op("act", lambda e, pi=pi, ni=ni, nk=nk: e.activation(out=pbuf[pi][:, 0:nk * 128], in_=nat[ni][:, 0:nk * 128], func=AF.Exp), reads=(nat_t[ni],), writes=(pb_t[pi],))
                return pi

            def n_pv(h, q4, pi):
                ot, oa = acc[("n", h)]
                kts = _na_kts(blk * 4 + q4)
                nk = len(kts)
                for ki, kt in enumerate(kts):
                    P.op("pe", lambda e, kt=kt, ki=ki, h=h, pi=pi, oa=oa, q4=q4, nk=nk: e.matmul(oa[:, q4 * 128:(q4 + 1) * 128], vaug[:, kt, 1 + h // 2, 64 * (h % 2):64 * (h % 2) + 128], pbuf[pi][:, ki * 128:(ki + 1) * 128], start=(ki == 0), stop=(ki == nk - 1)),
                         reads=(v_t[kt], pb_t[pi]), writes=(ot,))
                if q4 == 3:
                    finish(ot, oa, 4 + h // 2, h % 2)

            items = [("g", hq, kt) for hq in (0, 4, 1, 5, 2, 6, 3, 7) for kt in range(16)] + [("n", h, q4) for h in range(8) for q4 in range(4)]
            pend = []
            for ii, it in enumerate(items):
                if ii == 16:
                    q_pair(2)
                if ii == 40 and blk + 1 < NB:
                    norm_b(blk + 1)
                pi = g_qk(it[1], it[2]) if it[0] == "g" else n_qk(it[1], it[2])
                pend.append((it, pi))
                LA = 3 if it[0] == "g" else 2
                if len(pend) > LA:
                    (i0, p0) = pend.pop(0)
                    (g_pv if i0[0] == "g" else n_pv)(i0[1], i0[2], p0)
            for (i0, p0) in pend:
                (g_pv if i0[0] == "g" else n_pv)(i0[1], i0[2], p0)
            for d in range(8):
                wt, wa = ring.get(("ABO", j, d))
                pt, pa = psum()
                for pr in range(8):
                    P.op("pe", lambda e, pr=pr, wa=wa, pa=pa: e.matmul(pa, wa[:, pr * 128:(pr + 1) * 128], yb[:, pr, :], start=(pr == 0), stop=(pr == 7)),
                         reads=(wt, yb_t[2 * pr], yb_t[2 * pr + 1]), writes=(pt,))
                xs = xT[:, d, blk * TB:(blk + 1) * TB]
                P.op("dve", lambda e, xs=xs, pa=pa: e.tensor_tensor(out=xs, in0=pa, in1=xs, op=ALU.add),
                     reads=(pt, xt_tiles[d][blk]), writes=(xt_tiles[d][blk],))
        P.barrier()

    for s in range(nseq):
        load_x(s)
        for st in stages:
            kind = st[0]
            if kind == "ffn":
                ffn(st[1], st[2])
            elif kind == "mix":
                if st[1] % 2 == 0:
                    mixer_ab(st[1])
                else:
                    mixer_cd(st[1])
        if not cfg.get("no_store"):
            store_out(s)
    return ring.rec


ALL_STAGES = []
for _i in range(DEPTH):
    ALL_STAGES += [("ffn", _i, 0), ("mix", _i), ("ffn", _i, 1)]


def build_nc(cfg):
    nseq = cfg["nseq"]
    nc = bass.Bass("TRN2", target_bir_lowering=False)
    T = {}
    T["x_d"] = nc.dram_tensor("x", [nseq * S, D], F32, kind="ExternalInput").ap()
    T["out_d"] = nc.dram_tensor("out", [nseq * S, D], F32, kind="ExternalOutput").ap()
    T["wall"] = nc.dram_tensor("wall", [128, NCW_PAD], F32, kind="ExternalInput").ap()
    T["pvec_d"] = nc.dram_tensor("pvec", [128, NPV], F32, kind="ExternalInput").ap()
    T["cmat_d"] = nc.dram_tensor("cmat", [128, NCM], F32, kind="ExternalInput").ap()
    T["rope_d"] = nc.dram_tensor("rope", [2, 128, S], F32, kind="ExternalInput").ap()
    T["bias_d"] = nc.dram_tensor("nabias", [2, 8, 5, 128, 640], F32, kind="ExternalInput").ap()
    T["wbf"] = nc.dram_tensor("wbf", [128, NCW_PAD], BF16, kind="Internal").ap()
    AW = cfg.get("AW", 29700)
    T["AW"] = AW
    import contextlib
    with contextlib.ExitStack() as es:
        xT_t = es.enter_context(nc.sbuf_tensor("xT", [128, 8 * S], F32))
        ring_t = es.enter_context(nc.sbuf_tensor("ring", [128, NSLOT * SLOT], BF16))
        pvec_t = es.enter_context(nc.sbuf_tensor("pvec_sb", [128, NPV], F32))
        cmat_t = es.enter_context(nc.sbuf_tensor("cmat_sb", [128, NCM], F32))
        cbf_t = es.enter_context(nc.sbuf_tensor("cbf_sb", [128, 384], BF16))
        arena_t = es.enter_context(nc.sbuf_tensor("arena", [128, AW], F32))
        cvi_t = es.enter_context(nc.sbuf_tensor("cvt_in", [128, 3 * CVT], F32))
        cvo_t = es.enter_context(nc.sbuf_tensor("cvt_out", [128, 3 * CVT], BF16))
        T["cvt_in"] = cvi_t[:, :]
        T["cvt_out"] = cvo_t[:, :]
        pss = [es.enter_context(nc.psum_tensor("ps%d" % k, [128, 512], F32)) for k in range(8)]
        T["xT"] = xT_t[:, :].rearrange("p (c t) -> p c t", c=8)
        T["ring"] = ring_t[:, :]
        T["pvec"] = pvec_t[:, :]
        T["cmat"] = cmat_t[:, :]
        T["cbf"] = cbf_t[:, :]
        T["arena"] = arena_t[:, :]
        T["ps"] = [p[:, :] for p in pss]
        dry = Prog()
        rec = _emit_program(dry, T, cfg, None)
        P = Prog()
        _emit_program(P, T, cfg, rec)
        sems = {e: es.enter_context(nc.semaphore("s_" + e)) for e in ENGS}
        for d in P.dsems:
            d.h = es.enter_context(nc.semaphore("d%d" % d.idx))
        block = es.enter_context(nc.Block())

        def end_fn(e, eng):
            if e == "sp":
                for d in P.dsems:
                    if d.count:
                        eng.wait_ge(d.h, d.count)

        P.emit(nc, block, sems, end_fn)
    return nc


def _prep_shared(inputs):
    inp = {k: np.asarray(v) for k, v in inputs.items()}
    wall = _pack_weights(inp)
    pv = _pack_pvec(inp)
    cm = _const_mats()
    rope = _rope_tables()
    nab = _na_bias(inp["b_rpb"].astype(np.float32)).reshape(2, 8, 5, 128, 640)
    return {"wall": wall, "pvec": pv, "cmat": cm, "rope": rope, "nabias": nab}


def kernel(**inputs):
    x = np.asarray(inputs["x"], dtype=np.float32)
    B = x.shape[0]
    per = B // NCORES
    shared = _prep_shared(inputs)
    cfg = {"nseq": per, "stages": ALL_STAGES}
    nc = build_nc(cfg)
    in_maps = []
    for c in range(NCORES):
        m = dict(shared)
        m["x"] = np.ascontiguousarray(x[c * per:(c + 1) * per].reshape(per * S, D))
        in_maps.append(m)
    res = run_bass_kernel_spmd(nc, in_maps, core_ids=list(range(NCORES)))
    out = np.concatenate([np.asarray(r["out"]).reshape(per, S, D) for r in res.results], axis=0)
    return out.astype(np.float32)
```
